# Optimizing a Trainium2 kernel written in Bass

```python
import math
import jax, jax.numpy as jnp
from jax import lax
import numpy as np

D_MODEL = 1024
BATCH = 16
SEQ = 2048
DEPTH = 2
DEC_BATCH = 8
DEC_SEQ = 16
PAST_LEN = 1024

CHUNK = 64
MIX_WIDTH = 2 * D_MODEL
GROUP_WIDTH = MIX_WIDTH // 4
EPS = 1e-6
NEG_INIT = -1e30

M_HEADS = 4
M_HD = GROUP_WIDTH // M_HEADS
M_QK = M_HD // 2
R_HEADS = 4
R_HD = GROUP_WIDTH // R_HEADS
R_QK = R_HD // 2
ROPE_BASE = 10000.0
S_HD = 64
S_HEADS = GROUP_WIDTH // S_HD
S_GROUPS = 2
S_STATE = 128
CONV_W = 4
S_CONV_DIM = GROUP_WIDTH + 2 * S_GROUPS * S_STATE
H_HEADS = 4
H_HD = GROUP_WIDTH // H_HEADS
H_KD = 128
H_KW = H_HEADS * H_KD
H_BLOCK = 16

IN_SIZES = (
    M_HEADS * M_QK, M_HEADS * M_QK, GROUP_WIDTH, GROUP_WIDTH, GROUP_WIDTH, 2 * M_HEADS,
    R_HEADS * R_QK, R_HEADS * R_QK, GROUP_WIDTH, GROUP_WIDTH,
    GROUP_WIDTH, S_CONV_DIM, S_HEADS,
    H_KW, H_KW, GROUP_WIDTH, GROUP_WIDTH,
)
IN_COLS = sum(IN_SIZES)

kernel_name = "hybrid_stream_encoder_step"


def rmsnorm(x, w):
    xf = x.astype(jnp.float32)
    y = xf * lax.rsqrt(jnp.mean(xf * xf, axis=-1, keepdims=True) + EPS)
    return (y * w.astype(jnp.float32)).astype(x.dtype)


def head_rms(h):
    return h * lax.rsqrt(jnp.mean(h * h, axis=-1, keepdims=True) + EPS)


def head_layernorm(h):
    c = h - jnp.mean(h, axis=-1, keepdims=True)
    return c * lax.rsqrt(jnp.mean(c * c, axis=-1, keepdims=True) + EPS)


def block_len(t, l):
    return l if t % l == 0 else t


def to_blocks(a, l):
    b, t = a.shape[:2]
    a = a.reshape((b, t // l, l) + a.shape[2:])
    return jnp.moveaxis(a, (1, 0, 3, 2), (0, 1, 2, 3))


def from_blocks(a):
    nc, b, h, l = a.shape[:4]
    a = jnp.moveaxis(a, (0, 1, 2, 3), (1, 0, 3, 2))
    return a.reshape((b, nc * l, h) + a.shape[4:])


def causal_tril(l):
    return jnp.tril(jnp.ones((l, l), dtype=bool))


def rotary(x, pos):
    half = x.shape[-1] // 2
    freqs = ROPE_BASE ** (-jnp.arange(half, dtype=jnp.float32) / half)
    ang = pos.astype(jnp.float32)[:, None] * freqs[None, :]
    cos = jnp.cos(ang)[None, :, None, :]
    sin = jnp.sin(ang)[None, :, None, :]
    x1, x2 = x[..., :half], x[..., half:]
    return jnp.concatenate([x1 * cos - x2 * sin, x1 * sin + x2 * cos], axis=-1)


def causal_dwconv(u, buf, w, b):
    up = jnp.concatenate([buf.astype(u.dtype), u], axis=1)
    y = lax.conv_general_dilated(up, w.astype(u.dtype)[:, None, :], window_strides=(1,), padding="VALID",
                                 dimension_numbers=("NWC", "WIO", "NWC"), feature_group_count=u.shape[-1])
    return jax.nn.silu(y + b.astype(u.dtype)), up[:, -(CONV_W - 1):]


def mlstm_block(carry, xs):
    c_prev, n_prev, m_prev = carry
    q, k, v, ig, lf = xs
    l = q.shape[2]
    b = jnp.cumsum(lf, axis=-1)
    d = jnp.where(causal_tril(l), b[..., :, None] - b[..., None, :] + ig[..., None, :], -jnp.inf)
    inter = b + m_prev[..., None]
    m_t = jnp.maximum(inter, jnp.max(d, axis=-1))
    s = jnp.einsum("bhlk,bhsk->bhls", q, k) * jnp.exp(d - m_t[..., None])
    wi = jnp.exp(inter - m_t)
    num = jnp.einsum("bhls,bhsv->bhlv", s, v) + wi[..., None] * jnp.einsum("bhlk,bhkv->bhlv", q, c_prev)
    den = jnp.sum(s, axis=-1) + wi * jnp.einsum("bhlk,bhk->bhl", q, n_prev)
    h = num / jnp.maximum(jnp.abs(den), jnp.exp(-m_t))[..., None]
    g = b[..., -1:] - b + ig
    m_new = jnp.maximum(b[..., -1] + m_prev, jnp.max(g, axis=-1))
    wk = jnp.exp(g - m_new[..., None])
    wc = jnp.exp(b[..., -1] + m_prev - m_new)
    c_new = wc[..., None, None] * c_prev + jnp.einsum("bhs,bhsk,bhsv->bhkv", wk, k, v)
    n_new = wc[..., None] * n_prev + jnp.einsum("bhs,bhsk->bhk", wk, k)
    return (c_new, n_new, m_new), h


def retention_block(s_prev, xs):
    q, k, v = xs
    l = q.shape[2]
    lg = jnp.log1p(-(2.0 ** (-5.0 - jnp.arange(R_HEADS, dtype=jnp.float32))))
    idx = jnp.arange(l, dtype=jnp.float32)
    dec = jnp.exp(jnp.where(causal_tril(l), (idx[:, None] - idx[None, :])[None] * lg[:, None, None], -jnp.inf))
    s = jnp.einsum("bhlk,bhsk->bhls", q, k) * dec
    o = (jnp.einsum("bhls,bhsv->bhlv", s, v)
         + jnp.exp((idx[None] + 1.0) * lg[:, None])[..., None] * jnp.einsum("bhlk,bhkv->bhlv", q, s_prev))
    kw = k * jnp.exp((l - 1.0 - idx)[None] * lg[:, None])[..., None]
    s_new = jnp.exp(l * lg)[:, None, None] * s_prev + jnp.einsum("bhsk,bhsv->bhkv", kw, v)
    return s_new, o


def ssd_block(h_prev, xs):
    cq, bk, xv, la = xs
    bsz, g, l, n = cq.shape
    hg = S_HEADS // S_GROUPS
    b = jnp.cumsum(la, axis=-1)
    dec = jnp.exp(jnp.where(causal_tril(l), b[..., :, None] - b[..., None, :], -jnp.inf))
    cb = jnp.einsum("bgln,bgsn->bgls", cq, bk)
    scores = cb[:, :, None] * dec.reshape(bsz, g, hg, l, l)
    xg = xv.reshape(bsz, g, hg, l, S_HD)
    hp = h_prev.reshape(bsz, g, hg, n, S_HD)
    y = (jnp.einsum("bgkls,bgksp->bgklp", scores, xg)
         + jnp.exp(b).reshape(bsz, g, hg, l)[..., None] * jnp.einsum("bgln,bgknp->bgklp", cq, hp))
    xw = (xv * jnp.exp(b[..., -1:] - b)[..., None]).reshape(bsz, g, hg, l, S_HD)
    h_new = (jnp.exp(b[..., -1])[..., None, None] * h_prev
             + jnp.einsum("bgsn,bgksp->bgknp", bk, xw).reshape(bsz, S_HEADS, n, S_HD))
    return h_new, y.reshape(bsz, S_HEADS, l, S_HD)


def hgrn_block(s_prev, xs):
    q, k, v, lf = xs
    l = q.shape[2]
    b = jnp.cumsum(lf, axis=2)
    diff = b[:, :, :, None, :] - b[:, :, None, :, :]
    w = jnp.exp(jnp.where(causal_tril(l)[:, :, None], diff, -jnp.inf))
    s = jnp.einsum("bhtk,bhsk,bhtsk->bhts", q, k, w)
    o = jnp.einsum("bhts,bhsv->bhtv", s, v) + jnp.einsum("bhtk,bhkv->bhtv", q * jnp.exp(b), s_prev)
    bl = b[:, :, -1]
    s_new = jnp.exp(bl)[..., None] * s_prev + jnp.einsum("bhsk,bhsv->bhkv", k * jnp.exp(bl[:, :, None, :] - b), v)
    return s_new, o


def mixer_layer(x, pos, states, norm_w, w_in, m_gate_b, m_norm_w, conv_w, conv_b,
                dt_bias, a_log, d_skip, s_norm_w, lb, h_norm_w, w_out):
    f32 = jnp.float32
    bsz, t, _ = x.shape
    m_c, m_n, m_m, r_s, s_h, s_buf, h_s = [s.astype(f32) for s in states]
    hn = rmsnorm(x, norm_w)
    proj = jnp.einsum("btd,dc->btc", hn, w_in).astype(f32)
    cuts = np.cumsum(IN_SIZES)[:-1].tolist()
    (mq, mk, mv, mo, mz, mif, rq, rk, rv, rg, sz, sxbc, sdt, hq, hf, hi, hg) = jnp.split(proj, cuts, axis=-1)

    lc = block_len(t, CHUNK)
    q = mq.reshape(bsz, t, M_HEADS, M_QK) * (M_QK ** -0.5)
    k = mk.reshape(bsz, t, M_HEADS, M_QK)
    v = mv.reshape(bsz, t, M_HEADS, M_HD)
    gates = (mif + m_gate_b.astype(f32)).reshape(bsz, t, 2, M_HEADS)
    ig = gates[:, :, 0]
    lf = jax.nn.log_sigmoid(gates[:, :, 1])
    (m_c, m_n, m_m), hm = lax.scan(mlstm_block, (m_c, m_n, m_m),
                                   (to_blocks(q, lc), to_blocks(k, lc), to_blocks(v, lc),
                                    to_blocks(ig, lc), to_blocks(lf, lc)))
    hm = head_rms(from_blocks(hm)) * m_norm_w.astype(f32).reshape(M_HEADS, M_HD)
    out_m = hm.reshape(bsz, t, GROUP_WIDTH) * jax.nn.sigmoid(mo) * jax.nn.silu(mz)

    q = rotary(rq.reshape(bsz, t, R_HEADS, R_QK), pos) * (R_QK ** -0.5)
    k = rotary(rk.reshape(bsz, t, R_HEADS, R_QK), pos)
    v = rv.reshape(bsz, t, R_HEADS, R_HD)
    r_s, hr = lax.scan(retention_block, r_s, (to_blocks(q, lc), to_blocks(k, lc), to_blocks(v, lc)))
    out_r = head_layernorm(from_blocks(hr)).reshape(bsz, t, GROUP_WIDTH) * jax.nn.silu(rg)

    xbc, s_buf = causal_dwconv(sxbc, s_buf, conv_w, conv_b)
    xs_, bs, cs = jnp.split(xbc, [GROUP_WIDTH, GROUP_WIDTH + S_GROUPS * S_STATE], axis=-1)
    xs_ = xs_.reshape(bsz, t, S_HEADS, S_HD)
    bs = bs.reshape(bsz, t, S_GROUPS, S_STATE)
    cs = cs.reshape(bsz, t, S_GROUPS, S_STATE)
    dt = jax.nn.softplus(sdt + dt_bias.astype(f32))
    la = dt * (-jnp.exp(a_log.astype(f32)))
    s_h, ys = lax.scan(ssd_block, s_h, (to_blocks(cs, lc), to_blocks(bs, lc),
                                        to_blocks(xs_ * dt[..., None], lc), to_blocks(la, lc)))
    ys = from_blocks(ys) + d_skip.astype(f32)[:, None] * xs_
    ys = (ys.reshape(bsz, t, GROUP_WIDTH) * jax.nn.silu(sz)).reshape(bsz, t, S_GROUPS, GROUP_WIDTH // S_GROUPS)
    out_s = head_rms(ys).reshape(bsz, t, GROUP_WIDTH) * s_norm_w.astype(f32)

    lh = block_len(t, H_BLOCK)
    q = hq.reshape(bsz, t, H_HEADS, H_KD) * (H_KD ** -0.5)
    k = ((1.0 - lb) * jax.nn.sigmoid(-hf)).reshape(bsz, t, H_HEADS, H_KD)
    lf = jnp.log(lb + (1.0 - lb) * jax.nn.sigmoid(hf)).reshape(bsz, t, H_HEADS, H_KD)
    v = hi.reshape(bsz, t, H_HEADS, H_HD)
    h_s, ho = lax.scan(hgrn_block, h_s, (to_blocks(q, lh), to_blocks(k, lh), to_blocks(v, lh), to_blocks(lf, lh)))
    ho = head_rms(from_blocks(ho)) * h_norm_w.astype(f32).reshape(H_HEADS, H_HD)
    out_h = ho.reshape(bsz, t, GROUP_WIDTH) * jax.nn.silu(hg)

    mix = jnp.concatenate([out_m, out_r, out_s, out_h], axis=-1).astype(x.dtype)
    y = x + jnp.einsum("btc,cd->btd", mix, w_out)
    return y, (m_c, m_n, m_m, r_s, s_h, s_buf, h_s)


def trunk(x, pos, layer_states, lbs, norm_w, w_in, mlstm_gate_b, mlstm_norm_w, ssd_conv_w, ssd_conv_b,
          ssd_dt_bias, ssd_a_log, ssd_d, ssd_norm_w, hgrn_norm_w, w_out, final_norm_w):
    new = []
    for l in range(DEPTH):
        x, st = mixer_layer(x, pos, layer_states[l], norm_w[l], w_in[l], mlstm_gate_b[l], mlstm_norm_w[l],
                            ssd_conv_w[l], ssd_conv_b[l], ssd_dt_bias[l], ssd_a_log[l], ssd_d[l],
                            ssd_norm_w[l], lbs[l], hgrn_norm_w[l], w_out[l])
        new.append(st)
    stacked = [jnp.stack([st[i] for st in new], axis=0) for i in range(7)]
    return rmsnorm(x, final_norm_w), stacked


def setup_inputs(seed: int = 0) -> dict:
    key = jax.random.key(seed)
    ks = jax.random.split(key, 24)
    f32 = jnp.float32

    def nrm(k, shape, s=1.0):
        return s * jax.random.normal(k, shape, f32)

    dt = jnp.exp(jax.random.uniform(ks[14], (DEPTH, S_HEADS), f32, math.log(1e-3), math.log(1e-1)))
    gate_b = jnp.concatenate([nrm(ks[11], (DEPTH, M_HEADS), 0.1),
                              jnp.linspace(3.0, 6.0, M_HEADS, dtype=f32)[None] + nrm(ks[12], (DEPTH, M_HEADS), 0.1)],
                             axis=1)
    return {
        "x_prompt": nrm(ks[0], (BATCH, SEQ, D_MODEL)),
        "x_sample": nrm(ks[1], (DEC_BATCH, DEC_SEQ, D_MODEL)),
        "state_mlstm_c": nrm(ks[2], (DEPTH, DEC_BATCH, M_HEADS, M_QK, M_HD)),
        "state_mlstm_n": nrm(ks[3], (DEPTH, DEC_BATCH, M_HEADS, M_QK)),
        "state_mlstm_m": nrm(ks[4], (DEPTH, DEC_BATCH, M_HEADS)),
        "state_ret": nrm(ks[5], (DEPTH, DEC_BATCH, R_HEADS, R_QK, R_HD)),
        "state_ssd": nrm(ks[6], (DEPTH, DEC_BATCH, S_HEADS, S_STATE, S_HD), 0.5),
        "cache_ssd_conv": nrm(ks[7], (DEPTH, DEC_BATCH, CONV_W - 1, S_CONV_DIM)),
        "state_hgrn": nrm(ks[8], (DEPTH, DEC_BATCH, H_HEADS, H_KD, H_HD)),
        "norm_w": 1.0 + nrm(ks[9], (DEPTH, D_MODEL), 0.1),
        "w_in": nrm(ks[10], (DEPTH, D_MODEL, IN_COLS), D_MODEL ** -0.5),
        "mlstm_gate_b": gate_b,
        "mlstm_norm_w": 1.0 + nrm(ks[13], (DEPTH, GROUP_WIDTH), 0.1),
        "ssd_conv_w": nrm(ks[15], (DEPTH, CONV_W, S_CONV_DIM), CONV_W ** -0.5),
        "ssd_conv_b": nrm(ks[16], (DEPTH, S_CONV_DIM), 0.01),
        "ssd_dt_bias": dt + jnp.log(-jnp.expm1(-dt)),
        "ssd_a_log": jnp.log(jax.random.uniform(ks[17], (DEPTH, S_HEADS), f32, 1.0, 16.0)),
        "ssd_d": 1.0 + nrm(ks[18], (DEPTH, S_HEADS), 0.1),
        "ssd_norm_w": 1.0 + nrm(ks[19], (DEPTH, GROUP_WIDTH), 0.1),
        "hgrn_lower_bounds": nrm(ks[20], (DEPTH, H_KW), 0.1),
        "hgrn_norm_w": 1.0 + nrm(ks[21], (DEPTH, GROUP_WIDTH), 0.1),
        "w_out": nrm(ks[22], (DEPTH, MIX_WIDTH, D_MODEL), MIX_WIDTH ** -0.5),
        "final_norm_w": 1.0 + nrm(ks[23], (D_MODEL,), 0.1),
    }


def reference(x_prompt, x_sample, state_mlstm_c, state_mlstm_n, state_mlstm_m, state_ret, state_ssd,
              cache_ssd_conv, state_hgrn, norm_w, w_in, mlstm_gate_b, mlstm_norm_w, ssd_conv_w, ssd_conv_b,
              ssd_dt_bias, ssd_a_log, ssd_d, ssd_norm_w, hgrn_lower_bounds, hgrn_norm_w, w_out, final_norm_w):
    f32 = jnp.float32
    p = jax.nn.softmax(hgrn_lower_bounds.astype(f32), axis=0)
    lbs = jnp.cumsum(p, axis=0) - p[0]
    weights = (norm_w, w_in, mlstm_gate_b, mlstm_norm_w, ssd_conv_w, ssd_conv_b, ssd_dt_bias, ssd_a_log,
               ssd_d, ssd_norm_w, hgrn_norm_w, w_out, final_norm_w)

    bp, tp = x_prompt.shape[0], x_prompt.shape[1]
    empty = (jnp.zeros((bp, M_HEADS, M_QK, M_HD), f32), jnp.zeros((bp, M_HEADS, M_QK), f32),
             jnp.full((bp, M_HEADS), NEG_INIT, f32), jnp.zeros((bp, R_HEADS, R_QK, R_HD), f32),
             jnp.zeros((bp, S_HEADS, S_STATE, S_HD), f32), jnp.zeros((bp, CONV_W - 1, S_CONV_DIM), f32),
             jnp.zeros((bp, H_HEADS, H_KD, H_HD), f32))
    y_prompt, p_st = trunk(x_prompt, jnp.arange(tp), [empty] * DEPTH, lbs, *weights)

    ts = x_sample.shape[1]
    carried = [(state_mlstm_c[l], state_mlstm_n[l], state_mlstm_m[l], state_ret[l], state_ssd[l],
                cache_ssd_conv[l], state_hgrn[l]) for l in range(DEPTH)]
    y_sample, s_st = trunk(x_sample, PAST_LEN + jnp.arange(ts), carried, lbs, *weights)

    p_mlstm_c, p_mlstm_n, p_mlstm_m, p_ret, p_ssd, p_conv, p_hgrn = p_st
    s_mlstm_c, s_mlstm_n, s_mlstm_m, s_ret, s_ssd, s_conv, s_hgrn = s_st
    return (y_prompt, y_sample, p_mlstm_c, p_mlstm_n, p_mlstm_m, p_ret, p_ssd, p_conv, p_hgrn,
            s_mlstm_c, s_mlstm_n, s_mlstm_m, s_ret, s_ssd, s_conv, s_hgrn)
```

```python
import math
from contextlib import ExitStack
import numpy as np
import concourse.bass as bass
import concourse.mybir as mybir
from concourse.bass_utils import run_bass_kernel_spmd

F32 = mybir.dt.float32
BF16 = mybir.dt.bfloat16
AF = mybir.ActivationFunctionType
ALU = mybir.AluOpType
AX = mybir.AxisListType

D = 1024
EPS = 1e-6
INC = 7184
ENGS = ("pe", "act", "dve", "pool", "sp")


class Reg:
    __slots__ = ("w", "r")

    def __init__(self):
        self.w = None
        self.r = []


class Op:
    __slots__ = ("eng", "fn", "deps", "dma", "signal", "sem", "semval", "prec", "seg", "win", "gi", "prio", "succ", "cost")

    def __init__(self, eng, fn, dma):
        self.eng = eng
        self.fn = fn
        self.deps = []
        self.dma = dma
        self.signal = False
        self.sem = None
        self.semval = 0
        self.prec = []
        self.seg = 0
        self.win = 0
        self.gi = 0
        self.prio = 0.0
        self.succ = []
        self.cost = 0.5


class T:
    def __init__(self, t):
        self.t = t
        self.reg = Reg()
        self.arena = False

    def __getitem__(self, k):
        return self.t[k]


COST = {"pe": 0.25, "act": 0.5, "dve": 0.6, "pool": 1.0, "sp": 0.1}
M_PIN = ("act", "pool")
SCHED_MODE = "segment"


class Prog:
    def __init__(self, nc, n_dma_sems=8):
        self.nc = nc
        self.ops = {e: [] for e in ENGS}
        self.n_dma_sems = n_dma_sems
        self.all = []
        self.seg = 0
        self.win = 0
        self.nosched = set()
        self.pins = {}

    def barrier(self):
        self.seg += 1
        self.win += 1

    def window(self, sched=True, pin=()):
        self.win += 1
        if not sched:
            self.nosched.add(self.win)
        if pin:
            self.pins[self.win] = tuple(pin)

    def emit(self, eng, fn, R=(), W=(), dma=False, cost=None):
        op = Op(eng, fn, dma)
        op.seg = self.seg
        op.win = self.win
        op.gi = len(self.all)
        op.cost = (2.0 if dma else COST[eng]) if cost is None else cost
        if dma and any(t_.arena for t_ in tuple(R) + tuple(W)):
            op.signal = True
            op.prio = -1.0
        deps = []
        for r in R:
            r = r.reg
            if r.w is not None:
                deps.append(r.w)
        for w in W:
            w = w.reg
            if w.w is not None:
                deps.append(w.w)
            deps.extend(w.r)
        seen = set()
        for d in deps:
            if d is op or id(d) in seen:
                continue
            seen.add(id(d))
            op.prec.append(d)
        for r in R:
            r.reg.r.append(op)
        for w in W:
            w.reg.w = op
            w.reg.r = []
        self.ops[eng].append(op)
        self.all.append(op)
        return op

    @staticmethod
    def _list_schedule(ops, pin=()):
        inwin = set(id(o) for o in ops)
        for o in ops:
            o.succ = []
        indeg = {}
        lastdma = {}
        for o in ops:
            n = 0
            for d in o.prec:
                if id(d) in inwin:
                    d.succ.append(o)
                    n += 1
            if o.dma or o.eng in pin:
                pd = lastdma.get(o.eng)
                if pd is not None and pd not in o.prec:
                    pd.succ.append(o)
                    n += 1
                lastdma[o.eng] = o
            indeg[id(o)] = n
        prio = {}
        for o in reversed(ops):
            best = 0.0
            for sc in o.succ:
                p_ = prio[id(sc)]
                if p_ > best:
                    best = p_
            prio[id(o)] = best + o.cost
        ready = [o for o in ops if indeg[id(o)] == 0]
        rtime = {id(o): 0.0 for o in ready}
        efree = {e: 0.0 for e in ENGS}
        out = []
        nleft = len(ops)
        while nleft:
            best = None
            bkey = None
            for o in ready:
                st_ = rtime[id(o)]
                ef = efree[o.eng]
                if ef > st_:
                    st_ = ef
                key = (st_, -prio[id(o)], o.gi)
                if bkey is None or key < bkey:
                    bkey = key
                    best = o
            o = best
            ready.remove(o)
            st_ = bkey[0]
            if o.dma:
                efree[o.eng] = st_ + 0.1
                f = st_ + o.cost
            else:
                f = st_ + o.cost
                efree[o.eng] = f
            out.append(o)
            nleft -= 1
            for sc in o.succ:
                indeg[id(sc)] -= 1
                if f > rtime.get(id(sc), 0.0):
                    rtime[id(sc)] = f
                if indeg[id(sc)] == 0:
                    ready.append(sc)
                    rtime.setdefault(id(sc), 0.0)
        return out

    def schedule(self):
        arena_flag = {id(o): (o.dma and o.prio == -1.0) for o in self.all}
        groups = []
        key = None
        for o in self.all:
            k = (o.seg, o.win if SCHED_MODE == "window" else 0)
            if k != key:
                groups.append([])
                key = k
            groups[-1].append(o)
        new_all = []
        for g in groups:
            if SCHED_MODE == "none" or len(g) < 3 or any(o_.win in self.nosched for o_ in g):
                new_all.extend(g)
            else:
                pin_ = ()
                for o_ in g:
                    if o_.win in self.pins:
                        pin_ = self.pins[o_.win]
                        break
                new_all.extend(self._list_schedule(g, pin_))
        new_ops = {e: [] for e in ENGS}
        for o in new_all:
            new_ops[o.eng].append(o)
        nseg = self.seg + 1
        last_compute = {}
        seg_first = {}
        seg_last = {}
        seg_arena = {}
        for e in ENGS:
            for o in new_ops[e]:
                if (o.seg, e) not in seg_first:
                    seg_first[(o.seg, e)] = o
                if not o.dma:
                    seg_last[(o.seg, e)] = o
                elif arena_flag[id(o)]:
                    seg_arena.setdefault(o.seg, []).append(o)
        carry = {}
        for k in range(nseg):
            fence = list(carry.values()) + seg_arena.get(k - 1, [])
            if k > 0:
                for e in ENGS:
                    fo = seg_first.get((k, e))
                    if fo is not None:
                        fo.prec = list(fo.prec) + [d for d in fence if d is not fo]
            for e in ("pe", "act", "dve", "pool"):
                if (k, e) in seg_last:
                    carry[e] = seg_last[(k, e)]
        self.ops = new_ops
        for e in ENGS:
            for op in self.ops[e]:
                op.deps = []
                op.signal = False
        pos = {}
        for e in ENGS:
            for k_, op in enumerate(self.ops[e]):
                pos[id(op)] = k_
        for e in ENGS:
            for op in self.ops[e]:
                seen = set()
                latest = {}
                for d in op.prec:
                    if id(d) in seen or d is op:
                        continue
                    seen.add(id(d))
                    if d.dma:
                        op.deps.append(d)
                        continue
                    if d.eng == op.eng and op.eng == "pe":
                        continue
                    cur = latest.get(d.eng)
                    if cur is None or pos[id(d)] > pos[id(cur)]:
                        latest[d.eng] = d
                for d in latest.values():
                    op.deps.append(d)
                    d.signal = True

    def finalize(self, stack, fin_cb=None):
        nc = self.nc
        self.schedule()
        if fin_cb is not None:
            fin_cb()
        engmap = {"pe": nc.tensor, "act": nc.scalar, "dve": nc.vector, "pool": nc.gpsimd, "sp": nc.sync}
        esem = {e: stack.enter_context(nc.semaphore("es_" + e)) for e in ENGS}
        dsem = {e: [stack.enter_context(nc.semaphore("ds_%s_%d" % (e, i))) for i in range(self.n_dma_sems)]
                for e in ("sp", "pool", "act")}
        dcount = {e: [0] * self.n_dma_sems for e in dsem}
        for e in ENGS:
            c = 0
            k = 0
            for op in self.ops[e]:
                if op.dma:
                    j = k % self.n_dma_sems
                    k += 1
                    dcount[e][j] += 16
                    op.sem = dsem[e][j]
                    op.semval = dcount[e][j]
                elif op.signal:
                    c += 1
                    op.sem = esem[e]
                    op.semval = c
        block = stack.enter_context(nc.Block())

        def run(e):
            engine = engmap[e]
            known = {}
            for op in self.ops[e]:
                waits = {}
                for d in op.deps:
                    key = id(d.sem)
                    if known.get(key, 0) >= d.semval:
                        continue
                    if key not in waits or waits[key][1] < d.semval:
                        waits[key] = (d.sem, d.semval)
                if op.dma and op.semval > 16:
                    key = id(op.sem)
                    v = op.semval - 16
                    if known.get(key, 0) < v and (key not in waits or waits[key][1] < v):
                        waits[key] = (op.sem, v)
                for key, (s, v) in waits.items():
                    engine.wait_ge(s, v)
                    known[key] = v
                ins = op.fn()
                if ins is None:
                    continue
                if op.dma:
                    ins.then_inc(op.sem, 16)
                elif op.signal:
                    ins.then_inc(op.sem, 1)

        @block.tensor
        def _(t):
            run("pe")

        @block.scalar
        def _(t):
            run("act")

        @block.vector
        def _(t):
            run("dve")

        @block.gpsimd
        def _(t):
            run("pool")

        @block.sync
        def _(t):
            run("sp")


MIX = ("M", "R", "S", "H")
WCOLS = {"M": (0, 2056), "R": (2056, 1536), "S": (3592, 1544), "H": (5136, 2048)}
GROUPS = {
    "M": [(0, 512), (512, 512), (1024, 512), (1536, 512), (2048, 8)],
    "R": [(0, 512), (512, 512), (1024, 512)],
    "S": [(0, 512), (512, 512), (1024, 512), (1536, 8)],
    "H": [(0, 512), (512, 512), (1024, 512), (1536, 512)],
}


def build_program(NSEQ, TP, NCH, TS=16, PAST=1024, enabled=MIX):
    nc = bass.Bass("TRN2", target_bir_lowering=False)
    st = ExitStack()
    P = Prog(nc)

    def din(name, shape):
        return nc.dram_tensor(name, list(shape), F32, kind="ExternalInput").ap()

    def dout(name, shape):
        return nc.dram_tensor(name, list(shape), F32, kind="ExternalOutput").ap()

    NCT = TP // 128
    xp = din("xp", [NSEQ, TP, D])
    xs = din("xs", [TS, D])
    st_mc = din("st_mc", [2, 4, 64, 128]); st_mn = din("st_mn", [2, 4, 64]); st_mm = din("st_mm", [2, 4])
    st_r = din("st_r", [2, 4, 64, 128]); st_s = din("st_s", [2, 8, 128, 64])
    st_cv = din("st_cv", [2, 3, 1024]); st_h = din("st_h", [2, 4, 128, 128])
    w_norm = din("w_norm", [2, D]); w_in = din("w_in", [2, D, INC]); w_gb = din("w_gb", [2, 8])
    w_mnw = din("w_mnw", [2, 512]); w_cw = din("w_cw", [2, 4, 1024]); w_cb = din("w_cb", [2, 1024])
    w_dtb = din("w_dtb", [2, 8]); w_alog = din("w_alog", [2, 8]); w_sd = din("w_sd", [2, 8])
    w_snw = din("w_snw", [2, 512]); w_lb = din("w_lb", [2, 512]); w_hnw = din("w_hnw", [2, 512])
    w_out = din("w_out", [2, 2048, D]); w_fn = din("w_fn", [1, D])
    c_id = din("c_id", [128, 128]); c_tri = din("c_tri", [128, 128]); c_ust = din("c_ust", [128, 128])
    c_cos = din("c_cos", [128, NCT + 1, 32]); c_sin = din("c_sin", [128, NCT + 1, 32])
    c_gqk = din("c_gqk", [128, 8]); c_gl = din("c_gl", [64, 8])

    y_p = dout("y_p", [NSEQ, TP, D]); y_s = dout("y_s", [TS, D])
    NS1 = NSEQ + 1
    o_mc = dout("o_mc", [2, NS1, 4, 64, 128]); o_mn = dout("o_mn", [2, NS1, 4, 64]); o_mm = dout("o_mm", [2, NS1, 4])
    o_r = dout("o_r", [2, NS1, 4, 64, 128]); o_s = dout("o_s", [2, NS1, 8, 128, 64])
    o_cv = dout("o_cv", [2, NS1, 3, 1024]); o_h = dout("o_h", [2, NS1, 4, 128, 128])
    scr_m = nc.dram_tensor("scr_m", [2, NS1, 4], F32, kind="Internal").ap()

    def sb(name, shape, dt=F32):
        return T(st.enter_context(nc.sbuf_tensor(name, list(shape), dt)))

    phys_banks = [T(st.enter_context(nc.psum_tensor("bank%d" % i, [128, 512], F32))) for i in range(8)]
    tick = {"n": 0}
    BANK_ROT = 0

    class BankMap:
        def __getitem__(self, k):
            return phys_banks[(k + BANK_ROT * (tick["n"] % 2)) % 8]

    banks = BankMap()

    def bfv(bank):
        return bank.t[:].bitcast(BF16)

    def E(eng, fn, R=(), W=()):
        return P.emit(eng, fn, R, W)

    def dma(q, out, in_, R=(), W=(), nonc=False):
        engine = {"sp": nc.sync, "pool": nc.gpsimd, "act": nc.scalar}[q]
        if nonc:
            return P.emit(q, lambda: engine.dma_start(out=out, in_=in_, allow_slow_non_contiguous=True), R, W, dma=True)
        return P.emit(q, lambda: engine.dma_start(out=out, in_=in_), R, W, dma=True)

    def mm(out, lhsT, rhs, start, stop, R, W):
        return E("pe", lambda: nc.tensor.matmul(out, lhsT=lhsT, rhs=rhs, start=start, stop=stop), R, W)

    def tr(out, in_, ident, R, W):
        return E("pe", lambda: nc.tensor.transpose(out, in_, ident), R, W)

    def act(out, in_, func, R, W, bias=None, scale=None, accum=None):
        kw = {}
        if bias is not None:
            kw["bias"] = bias
        if scale is not None:
            kw["scale"] = scale
        if accum is not None:
            kw["accum_out"] = accum
        return E("act", lambda: nc.scalar.activation(out=out, in_=in_, func=func, **kw), R, W)

    def vcopy(eng, out, in_, R, W):
        e = {"dve": nc.vector, "pool": nc.gpsimd}[eng]
        return E(eng, lambda: e.tensor_copy(out=out, in_=in_), R, W)

    def tt(eng, out, a, b, op, R, W):
        e = {"dve": nc.vector, "pool": nc.gpsimd}[eng]
        return E(eng, lambda: e.tensor_tensor(out=out, in0=a, in1=b, op=op), R, W)

    def ts(eng, out, a, s1, op0, R, W, s2=None, op1=None):
        e = {"dve": nc.vector, "pool": nc.gpsimd}[eng]
        if op1 is None:
            return E(eng, lambda: e.tensor_scalar(out=out, in0=a, scalar1=s1, scalar2=None, op0=op0), R, W)
        return E(eng, lambda: e.tensor_scalar(out=out, in0=a, scalar1=s1, scalar2=s2, op0=op0, op1=op1), R, W)

    def stt(eng, out, a, s, b, op0, op1, R, W):
        e = {"dve": nc.vector, "pool": nc.gpsimd}[eng]
        return E(eng, lambda: e.scalar_tensor_tensor(out=out, in0=a, scalar=s, in1=b, op0=op0, op1=op1), R, W)

    def memset(eng, t_, ap, val):
        e = {"dve": nc.vector, "pool": nc.gpsimd}[eng]
        return E(eng, lambda: e.memset(ap, val), (), (t_,))

    ident_f = sb("ident_f", [128, 128]); ident = sb("ident", [128, 128], BF16)
    tri_f = sb("tri_f", [128, 128]); tri = sb("tri", [128, 128], BF16)
    ust = sb("ust", [128, 128], BF16); ust_f = sb("ust_f", [128, 128])
    ones = sb("ones", [128, 128], BF16)
    cosT = sb("cosT", [128, NCT + 1, 32]); sinT = sb("sinT", [128, NCT + 1, 32])
    gqk = sb("gqk", [128, 8]); gl = sb("gl", [64, 8])
    dma("sp", ident_f[:], c_id, W=(ident_f,)); dma("sp", tri_f[:], c_tri, W=(tri_f,)); dma("sp", ust_f[:], c_ust, W=(ust_f,))
    dma("sp", cosT[:], c_cos, W=(cosT,)); dma("sp", sinT[:], c_sin, W=(sinT,))
    dma("sp", gqk[:], c_gqk, W=(gqk,)); dma("sp", gl[:], c_gl, W=(gl,))
    vcopy("dve", ident[:], ident_f[:], (ident_f,), (ident,))
    vcopy("dve", tri[:], tri_f[:], (tri_f,), (tri,))
    vcopy("dve", ust[:], ust_f[:], (ust_f,), (ust,))
    memset("dve", ones, ones[:], 1.0)

    cwall = sb("cwall", [128, 2, 8, 4]); cball = sb("cball", [128, 2, 8]); cbneg = sb("cbneg", [128, 2, 8])
    for l_ in range(2):
        for w_ in range(4):
            dma("sp", cwall[:, l_, :, w_:w_ + 1], w_cw[l_, w_].rearrange("(a p o) -> p a o", p=128, o=1), W=(cwall,), nonc=True)
        dma("sp", cball[:, l_, :].unsqueeze(2), w_cb[l_].rearrange("(a p o) -> p a o", p=128, o=1), W=(cball,), nonc=True)
    ts("dve", cbneg[:], cball[:], -1.0, ALU.mult, (cball,), (cbneg,))

    NU = NCH
    x_t = [sb("x%d" % i, [128, D]) for i in range(NU)]
    hnT = sb("hnT", [128, 8, NU * 128], BF16)
    hnT_r = [T(None) for _ in range(NU)]
    wslot = [sb("wslot%d" % i, [128, 8, 2056], BF16) for i in range(2)]
    woslot = [sb("woslot%d" % i, [128, 4, D], BF16) for i in range(2)]
    aux = [sb("aux%d" % i, [128, 1536]) for i in range(2)]
    saux = [sb("saux%d" % i, [128, 32]) for i in range(2)]
    junk = sb("junk", [128, D], BF16)

    Cst = [sb("Cst%d" % l, [64, 4, 129]) for l in range(2)]; Cbf = sb("Cbf", [64, 4, 129], BF16)
    Rst = [sb("Rst%d" % l, [64, 4, 128]) for l in range(2)]; Rbf = sb("Rbf", [64, 4, 128], BF16)
    Hst = [sb("Hst%d" % l, [128, 8, 64]) for l in range(2)]; Hbf = sb("Hbf", [128, 8, 64], BF16)
    Gst = [sb("Gst%d" % l, [128, 4, 128]) for l in range(2)]; Gbf = sb("Gbf", [128, 4, 128], BF16)
    xcT = [sb("xcT%d" % l, [128, 8, 131], BF16) for l in range(2)]
    mG = [sb("mG%d" % l, [4, NCT + 1]) for l in range(2)]
    mB = [sb("mB%d" % l, [4, NCT + 1]) for l in range(2)]
    m0 = [sb("m0_%d" % l, [4, 1]) for l in range(2)]

    ARENA_B = 43072 + 5184
    arena = st.enter_context(nc.sbuf_tensor("arena", [128, ARENA_B // 2], BF16))

    class Lay:
        def __init__(self, base):
            self.off = base

        def take(self, shape, dt=F32):
            n = 1
            for d_ in shape[1:]:
                n *= d_
            nb = n * (4 if dt == F32 else 2)
            a_ = arena[0:shape[0], self.off // 2:(self.off + nb) // 2]
            if dt == F32:
                a_ = a_.bitcast(F32)
            if len(shape) == 3:
                a_ = a_.rearrange("p (a b) -> p a b", b=shape[2])
            elif len(shape) == 4:
                a_ = a_.rearrange("p (a b c) -> p a b c", b=shape[2], c=shape[3])
            self.off += (nb + 63) // 64 * 64
            assert self.off <= ARENA_B, (self.off, ARENA_B)
            t_ = T(a_)
            t_.arena = True
            return t_

        def take2(self, shape, dt=F32):
            t_ = self.take(shape, dt)
            return [t_, t_]

    smsets = [[sb("sm%d_%d" % (i, j), [128, 64]) for i in range(4)] for j in range(2)]
    sm = smsets[0]
    smb = [sb("smb%d" % j, [128, 64], BF16) for j in range(2)]
    tstat = sb("tstat", [8, 132])
    lc = Lay(0)
    qk_b = [lc.take([128, 512], BF16) for _ in range(2)]
    v_b = [lc.take([128, 4, 129], BF16) for _ in range(2)]
    e1 = lc.take([128, 1024]); e2 = lc.take([128, 512]); g1 = lc.take2([128, 512])
    PT = [lc.take([128, 4, 128], BF16) for _ in range(2)]
    om = [lc.take([128, 512], BF16) for _ in range(2)]
    omT = [lc.take([128, 4, 128], BF16) for _ in range(2)]
    base = lc.off
    ln_ = Lay(base)
    hn_b = ln_.take2([128, D], BF16)
    normw = ln_.take([128, D])
    lm = Lay(base)
    qkT = lm.take2([64, 8, 128], BF16)
    rot = lm.take([128, 8, 64]); rt1 = lm.take([128, 8, 32]); rt2 = lm.take([128, 8, 32])
    ls = Lay(base)
    xbc_b = ls.take([128, 1024], BF16)
    cdiag = ls.take([128, 32, 128], BF16)
    xact = ls.take2([128, 8, 128], BF16)
    xb_tok = ls.take2([128, 1024], BF16)
    larhs = ls.take([128, 2, 8, 128], BF16)
    decT = ls.take([128, 8, 128])
    cbm = ls.take([128, 2, 128])
    scT = ls.take2([128, 8, 128], BF16)
    xdt = ls.take2([128, 512], BF16); xw = ls.take2([128, 512], BF16)
    yin = ls.take([128, 512])
    lh_ = Lay(base)
    kf = lh_.take([128, 512]); lfh = lh_.take([128, 2, 512], BF16)
    bT = lh_.take([128, 4, 128]); beta = lh_.take([128, 4, 4])
    dq = lh_.take([128, 4, 128]); dk = lh_.take([128, 4, 128])
    kT_b = lh_.take([128, 4, 128], BF16); qT_f = lh_.take([128, 4, 128])
    qe = lh_.take2([128, 4, 128], BF16); qb = lh_.take2([128, 4, 128], BF16)
    ke = [lh_.take([128, 4, 128], BF16) for i in range(4)]
    kd_b = lh_.take2([128, 512], BF16)
    kb_t = lh_.take([128, 512], BF16)
    pass

    units = []
    for s in range(NSEQ):
        for u0 in range(0, NCT, NCH):
            ch = []
            for j in range(NCH):
                ci = u0 + j
                if ci >= NCT:
                    break
                ch.append(dict(seq=s, L=128, slot=j, ci=ci, first=(ci == 0), last=(ci == NCT - 1), pidx=ci))
            units.append(ch)
    units = [[dict(seq=NSEQ, L=TS, slot=0, ci=0, first=True, last=True, pidx=NCT)]] + units

    wq = {"n": 0}

    def load_weights(l, m):
        i = wq["n"] % 2
        wq["n"] += 1
        c0, ncol = WCOLS[m]
        src = w_in[l].rearrange("(k p) c -> p k c", p=128)
        half = (ncol + 1) // 2
        dma("pool", wslot[i][:, :, 0:half], src[:, :, c0:c0 + half], W=(wslot[i],))
        dma("pool", wslot[i][:, :, half:ncol], src[:, :, c0 + half:c0 + ncol], W=(wslot[i],))
        mi = MIX.index(m)
        srco = w_out[l, mi * 512:(mi + 1) * 512, :].rearrange("(k p) c -> p k c", p=128)
        dma("pool", woslot[i][:], srco, W=(woslot[i],))
        a = aux[i]
        if m == "M":
            dma("sp", a[:, 0:512], w_mnw[l:l + 1, :].partition_broadcast(128), W=(a,))
            dma("sp", a[:, 512:520], w_gb[l:l + 1, :].partition_broadcast(128), W=(a,))
        elif m == "S":
            dma("sp", a[:, 0:512], w_snw[l:l + 1, :].partition_broadcast(128), W=(a,))
            sa = saux[i]
            dma("sp", sa[:, 0:8], w_dtb[l:l + 1, :].partition_broadcast(128), W=(sa,))
            dma("sp", sa[:, 8:16], w_alog[l:l + 1, :].partition_broadcast(128), W=(sa,))
            dma("sp", sa[:, 16:24], w_sd[l:l + 1, :].partition_broadcast(128), W=(sa,))
            dma("sp", a[:, 512:1536], w_cb[l:l + 1, :].partition_broadcast(128), W=(a,))
        elif m == "H":
            dma("sp", a[:, 0:512], w_hnw[l:l + 1, :].partition_broadcast(128), W=(a,))
            if l == 1:
                dma("sp", a[:, 512:1024], w_lb[0:1, :].partition_broadcast(128), W=(a,))
                dma("sp", a[:, 1024:1536], w_lb[1:2, :].partition_broadcast(128), W=(a,))
        return i

    def alt():
        return tick["n"] % 2

    def next_chunk():
        tick["n"] += 1

    def norm_chunk(c, wbc, out_bf, ev):
        L = c["L"]; xt = x_t[c["slot"]]
        s = smsets[alt()][0]
        act(junk[:L, :], xt[:L, :], AF.Square, (xt,), (junk, s), accum=s[:L, 0:1])
        act(s[:L, 1:2], s[:L, 0:1], AF.Ln, (s,), (s,), bias=EPS, scale=1.0 / D)
        act(s[:L, 2:3], s[:L, 1:2], AF.Exp, (s,), (s,), scale=-0.5)
        stt("dve", out_bf[:L, :], xt[:L, :], s[:L, 2:3], wbc[:L, :], ALU.mult, ALU.mult, (xt, s, wbc), (out_bf,))

    def layer_norm_transpose(c):
        L = c["L"]; slot = c["slot"]
        hb = hn_b[alt()]
        norm_chunk(c, normw, hb, None)
        bk = banks[2]
        v = bfv(bk)
        for k in range(8):
            tr(v[:, k * 128:k * 128 + L], hb[:L, k * 128:(k + 1) * 128], ident[:L, :L], (hb, ident), (bk,))
        E("act", lambda: nc.scalar.copy(out=hnT[:, :, slot * 128:slot * 128 + L],
                                        in_=v[:, 0:1024].rearrange("p (k t) -> p k t", t=128)[:, :, 0:L]),
          (bk,), (hnT_r[slot],))

    def project(c, wi, g0, gn, bank):
        L = c["L"]; slot = c["slot"]
        for k in range(8):
            mm(bank[:L, 0:gn], hnT[:, k, slot * 128:slot * 128 + L], wslot[wi][:, k, g0:g0 + gn],
               k == 0, k == 7, (hnT_r[slot], wslot[wi]), (bank,))

    def out_proj(c, wi, omt, accb=(3, 4)):
        L = c["L"]; xt = x_t[c["slot"]]
        bk = banks[6]; v = bfv(bk)
        for k in range(4):
            tr(v[:, k * 128:k * 128 + L], omt[:L, k * 128:(k + 1) * 128], ident[:L, :L], (omt, ident), (bk,))
        oT = omT[alt()]
        E("act", lambda: nc.scalar.copy(out=oT[:, :, 0:L], in_=v[:, 0:512].rearrange("p (k t) -> p k t", t=128)[:, :, 0:L]),
          (bk,), (oT,))
        for g in range(2):
            bank = banks[accb[g]]
            for k in range(4):
                mm(bank[:L, :], oT[:, k, 0:L], woslot[wi][:, k, g * 512:(g + 1) * 512], k == 0, k == 3,
                   (oT, woslot[wi]), (bank,))
            tt("dve", xt[:L, g * 512:(g + 1) * 512], xt[:L, g * 512:(g + 1) * 512], bank[:L, :], ALU.add,
               (xt, bank), (xt,))

    def hilo(src_ap, L, n, dst, R):
        vcopy("dve", dst[:L, 0, 0:n], src_ap, R, (dst,))
        tt("dve", dst[:L, 1, 0:n], src_ap, dst[:L, 0, 0:n], ALU.subtract, tuple(R) + (dst,), (dst,))

    def mlstm_chunk(c, l, wi):
        L = c["L"]; a = aux[wi]; pidx = c["pidx"]
        qb_ = qk_b[alt()]; vb = v_b[alt()]
        s0, s1, s2, s3 = smsets[alt()]
        project(c, wi, 0, 512, banks[0])
        E("act", lambda b_=banks[0]: nc.scalar.activation(out=qb_[:L, 0:256], in_=b_[:L, 0:256], func=AF.Copy, scale=0.125), (banks[0],), (qb_,))
        E("act", lambda b_=banks[0]: nc.scalar.copy(out=qb_[:L, 256:512], in_=b_[:L, 256:512]), (banks[0],), (qb_,))
        project(c, wi, 2048, 8, banks[7])
        tt("dve", s0[:L, 0:8], banks[7][:L, 0:8], a[:L, 512:520], ALU.add, (banks[7], a), (s0,))
        act(s0[:L, 8:12], s0[:L, 4:8], AF.Exp, (s0,), (s0,), scale=-1.0)
        act(s0[:L, 12:16], s0[:L, 8:12], AF.Ln, (s0,), (s0,), bias=1.0)
        ts("dve", s0[:L, 16:20], s0[:L, 12:16], -1.0, ALU.mult, (s0,), (s0,))
        lh = smb[alt()]
        vcopy("dve", lh[:L, 0:4], s0[:L, 16:20], (s0,), (lh,))
        tt("dve", lh[:L, 4:8], s0[:L, 16:20], lh[:L, 0:4], ALU.subtract, (s0, lh), (lh,))
        bk = banks[7]
        mm(bk[:L, 16:20], tri[:L, :L], lh[:L, 0:4], True, False, (tri, lh), (bk,))
        mm(bk[:L, 16:20], tri[:L, :L], lh[:L, 4:8], False, True, (tri, lh), (bk,))
        mm(bk[:, 32:36], ones[:L, :], lh[:L, 0:4], True, False, (ones, lh), (bk,))
        mm(bk[:, 32:36], ones[:L, :], lh[:L, 4:8], False, True, (ones, lh), (bk,))
        vcopy("dve", s1[:, 0:4], bk[:, 32:36], (bk,), (s1,))
        vcopy("dve", s1[:L, 4:8], bk[:L, 16:20], (bk,), (s1,))
        act(s1[:L, 8:12], s1[:L, 4:8], AF.Exp, (s1,), (s1,))
        tt("dve", s1[:L, 12:16], s0[:L, 0:4], s1[:L, 4:8], ALU.subtract, (s0, s1), (s1,))
        act(s1[:L, 16:20], s1[:L, 12:16], AF.Exp, (s1,), (s1,))
        tt("dve", s1[:L, 20:24], s1[:L, 12:16], s1[:L, 0:4], ALU.add, (s1,), (s1,))
        act(s1[:64, 24:28], s1[:64, 0:4], AF.Exp, (s1,), (s1,))
        bk6 = banks[6]; v6 = bfv(bk6)
        vcopy("dve", lh[:L, 16:20], s1[:L, 20:24], (s1,), (lh,))
        tt("dve", lh[:L, 20:24], s1[:L, 20:24], lh[:L, 16:20], ALU.subtract, (s1, lh), (lh,))
        vcopy("dve", lh[:L, 24:28], s1[:L, 4:8], (s1,), (lh,))
        tt("dve", lh[:L, 28:32], s1[:L, 4:8], lh[:L, 24:28], ALU.subtract, (s1, lh), (lh,))
        for j in range(4):
            tr(v6[0:4, j * 128:j * 128 + L], lh[:L, 16 + 4 * j:20 + 4 * j], ident[:L, :L], (lh, ident), (bk6,))
        vcopy("dve", tstat[0:4, 0:L], v6[0:4, 0:L], (bk6,), (tstat,))
        tt("dve", tstat[0:4, 0:L], tstat[0:4, 0:L], v6[0:4, 128:128 + L], ALU.add, (tstat, bk6), (tstat,))
        E("dve", lambda: nc.vector.tensor_reduce(out=mG[l][0:4, pidx:pidx + 1], in_=tstat[0:4, 0:L], axis=AX.X, op=ALU.max),
          (tstat,), (mG[l],))
        vcopy("dve", tstat[0:4, 130:131], v6[0:4, 256 + L - 1:256 + L], (bk6,), (tstat,))
        tt("dve", mB[l][0:4, pidx:pidx + 1], tstat[0:4, 130:131], v6[0:4, 384 + L - 1:384 + L], ALU.add, (tstat, bk6), (mB[l],))
        project(c, wi, 512, 512, banks[1])
        tt("dve", vb[:L, :, 0:128], banks[1][:L, :].rearrange("p (h v) -> p h v", v=128),
           s1[:L, 16:20].unsqueeze(2).to_broadcast([L, 4, 128]), ALU.mult, (banks[1], s1), (vb,))
        vcopy("dve", vb[:L, :, 128:129], s1[:L, 16:20].unsqueeze(2), (s1,), (vb,))
        bkt = banks[2]; v2 = bfv(bkt)
        for j in range(8):
            tr(v2[0:64, j * 128:j * 128 + L], qb_[:L, j * 64:(j + 1) * 64], ident[:L, :L], (qb_, ident), (bkt,))
        qT = qkT[alt()]
        E("act", lambda: nc.scalar.copy(out=qT[:, :, 0:L], in_=v2[0:64, 0:1024].rearrange("p (j t) -> p j t", t=128)[:, :, 0:L]),
          (bkt,), (qT,))
        bs = banks[3]
        for h in range(4):
            mm(bs[:L, h * 128:h * 128 + L], qT[:, 4 + h, 0:L], qT[:, h, 0:L], True, True, (qT,), (bs,))
        pt = PT[alt()]
        tt("dve", pt[:L, :, 0:L], bs[:L, :].rearrange("p (h t) -> p h t", t=128)[:, :, 0:L],
           tri_f[:L, 0:L].unsqueeze(1).to_broadcast([L, 4, L]), ALU.mult, (bs, tri_f), (pt,))
        for h in range(4):
            bo = banks[4 + h // 2]; o0 = (h % 2) * 129
            mm(bo[:L, o0:o0 + 129], pt[:L, h, 0:L], vb[:L, h, :], True, False, (pt, vb), (bo,))
            mm(bo[:L, o0:o0 + 129], qT[:, h, 0:L], Cbf[:, h, :], False, True, (qT, Cbf), (bo,))
        for h in range(4):
            bu = banks[6 + h // 2]; o0 = (h % 2) * 129
            mm(bu[0:64, o0:o0 + 129], qb_[:L, 256 + h * 64:256 + (h + 1) * 64], vb[:L, h, :], True, True, (qb_, vb), (bu,))
        C = Cst[l]
        for hp in range(2):
            bu = banks[6 + hp]
            tt("dve", C[:, 2 * hp:2 * hp + 2, :], C[:, 2 * hp:2 * hp + 2, :],
               bu[0:64, 0:258].rearrange("p (h v) -> p h v", v=129), ALU.add, (C, bu), (C,))
        tt("dve", C[:, :, :], C[:, :, :], s1[:64, 24:28].unsqueeze(2).to_broadcast([64, 4, 129]), ALU.mult, (C, s1), (C,))
        E("act", lambda: nc.scalar.copy(out=Cbf[:], in_=C[:]), (C,), (Cbf,))
        for hp in range(2):
            bo = banks[4 + hp]
            vcopy("dve", s2[:L, 2 * hp:2 * hp + 2], bo[:L, 0:258].rearrange("p (h v) -> p h v", v=129)[:, :, 128], (bo,), (s2,))
        tt("dve", s2[:L, 4:8], s2[:L, 0:4], s1[:L, 8:12], ALU.mult, (s2, s1), (s2,))
        tt("dve", s2[:L, 8:12], s2[:L, 4:8], s2[:L, 4:8], ALU.mult, (s2,), (s2,))
        ts("dve", s2[:L, 8:12], s2[:L, 8:12], 1.0, ALU.max, (s2,), (s2,))
        act(s2[:L, 40:44], s2[:L, 8:12], AF.Ln, (s2,), (s2,))
        act(s2[:L, 12:16], s2[:L, 40:44], AF.Exp, (s2,), (s2,), scale=-0.5)
        tt("dve", s2[:L, 16:20], s2[:L, 12:16], s1[:L, 8:12], ALU.mult, (s2, s1), (s2,))
        for h in range(4):
            bo = banks[4 + h // 2]; o0 = (h % 2) * 129
            act(junk[:L, 0:128], bo[:L, o0:o0 + 128], AF.Square, (bo,), (junk, s3), accum=s3[:L, h:h + 1])
        tt("dve", s2[:L, 20:24], s2[:L, 16:20], s2[:L, 16:20], ALU.mult, (s2,), (s2,))
        tt("dve", s2[:L, 24:28], s2[:L, 20:24], s3[:L, 0:4], ALU.mult, (s2, s3), (s2,))
        act(s2[:L, 28:32], s2[:L, 24:28], AF.Ln, (s2,), (s2,), bias=EPS, scale=1.0 / 128)
        act(s2[:L, 32:36], s2[:L, 28:32], AF.Exp, (s2,), (s2,), scale=-0.5)
        tt("dve", s2[:L, 36:40], s2[:L, 32:36], s2[:L, 16:20], ALU.mult, (s2,), (s2,))
        gg = g1[alt()]
        for h in range(4):
            bo = banks[4 + h // 2]; o0 = (h % 2) * 129
            stt("dve", gg[:L, h * 128:(h + 1) * 128], bo[:L, o0:o0 + 128], s2[:L, 36 + h:37 + h],
                a[:L, h * 128:(h + 1) * 128], ALU.mult, ALU.mult, (bo, s2, a), (gg,))
        project(c, wi, 1024, 512, banks[7])
        act(e1[:L, 0:512], banks[7][:L, :], AF.Exp, (banks[7],), (e1,), scale=-1.0)
        project(c, wi, 1536, 512, banks[6])
        act(e1[:L, 512:1024], banks[6][:L, :], AF.Exp, (banks[6],), (e1,), scale=-1.0)
        act(e1[:L, :], e1[:L, :], AF.Ln, (e1,), (e1,), bias=1.0)
        tt("pool", e2[:L, :], e1[:L, 0:512], e1[:L, 512:1024], ALU.add, (e1,), (e2,))
        act(e2[:L, :], e2[:L, :], AF.Exp, (e2,), (e2,), scale=-1.0)
        tt("dve", e2[:L, :], e2[:L, :], banks[6][:L, :], ALU.mult, (e2, banks[6]), (e2,))
        omt = om[alt()]
        tt("dve", omt[:L, :], e2[:L, :], gg[:L, :], ALU.mult, (e2, gg), (omt,))
        out_proj(c, wi, omt, (4, 5))

    def mlstm_init(c, l):
        C = Cst[l]
        if c["seq"] < NSEQ:
            memset("pool", C, C[:], 0.0)
            memset("pool", Cbf, Cbf[:], 0.0)
            memset("pool", m0[l], m0[l][:], -1e30)
        else:
            dma("sp", C[:, :, 0:128], st_mc[l].rearrange("h k v -> k h v"), W=(C,))
            dma("sp", C[:, :, 128:129], st_mn[l].rearrange("h (k o) -> k h o", o=1), W=(C,), nonc=True)
            dma("sp", m0[l][:], st_mm[l].rearrange("(h o) -> h o", o=1), W=(m0[l],), nonc=True)
            s = sm[3]
            dma("sp", s[0:64, 32:36], st_mm[l:l + 1, :].partition_broadcast(64), W=(s,))
            act(s[0:64, 36:40], s[0:64, 32:36], AF.Exp, (s,), (s,))
            tt("dve", C[:, :, :], C[:, :, :], s[0:64, 36:40].unsqueeze(2).to_broadcast([64, 4, 129]), ALU.mult, (C, s), (C,))
            E("act", lambda: nc.scalar.copy(out=Cbf[:], in_=C[:]), (C,), (Cbf,))

    def mlstm_final(c, l, nchunks, p0):
        sq = c["seq"]; ts_ = tstat; n = nchunks
        G = mG[l]; B = mB[l]
        E("dve", lambda: nc.vector.tensor_tensor_scan(out=ts_[0:4, 0:n], data0=ones_f4[0:4, 0:n], data1=B[0:4, p0:p0 + n],
                                                     initial=0.0, op0=ALU.mult, op1=ALU.add), (B, ones_f4), (ts_,))
        tt("dve", ts_[0:4, 32:32 + n], G[0:4, p0:p0 + n], ts_[0:4, 0:n], ALU.subtract, (G, ts_), (ts_,))
        ts("dve", ts_[0:4, 32:32 + n], ts_[0:4, 32:32 + n], ts_[0:4, n - 1:n], ALU.add, (ts_,), (ts_,))
        E("dve", lambda: nc.vector.tensor_reduce(out=ts_[0:4, 64:65], in_=ts_[0:4, 32:32 + n], axis=AX.X, op=ALU.max), (ts_,), (ts_,))
        tt("dve", ts_[0:4, 65:66], m0[l][0:4, 0:1], ts_[0:4, n - 1:n], ALU.add, (m0[l], ts_), (ts_,))
        tt("dve", ts_[0:4, 66:67], ts_[0:4, 64:65], ts_[0:4, 65:66], ALU.max, (ts_,), (ts_,))
        dma("sp", o_mm[l, sq].rearrange("(h o) -> h o", o=1), ts_[0:4, 66:67], R=(ts_,), nonc=True)
        d1 = dma("sp", scr_m[l, sq].rearrange("(h o) -> h o", o=1), ts_[0:4, 66:67], R=(ts_,), W=(scr_reg,), nonc=True)
        s = sm[3]
        dma("sp", s[0:64, 40:44], scr_m[l, sq:sq + 1, :].partition_broadcast(64), R=(scr_reg,), W=(s,))
        act(s[0:64, 44:48], s[0:64, 40:44], AF.Exp, (s,), (s,), scale=-1.0)
        C = Cst[l]
        co = cout
        tt("dve", co[:, :, :], C[:, :, :], s[0:64, 44:48].unsqueeze(2).to_broadcast([64, 4, 129]), ALU.mult, (C, s), (co,))
        dma("sp", o_mc[l, sq].rearrange("h k v -> k h v"), co[:, :, 0:128], R=(co,))
        dma("sp", o_mn[l, sq].rearrange("h (k o) -> k h o", o=1), co[:, :, 128:129], R=(co,), nonc=True)

    ones_f4 = sb("ones_f4", [4, 32]); memset("dve", ones_f4, ones_f4[:], 1.0)
    scr_reg = T(None)
    cout = T(e1[0:64, 0:516].rearrange("p (h v) -> p h v", v=129)); cout.reg = e1.reg; cout.arena = True

    def ret_chunk(c, l, wi):
        L = c["L"]; pidx = c["pidx"]
        project(c, wi, 0, 512, banks[0])
        b0 = banks[0]
        x4 = b0[:L, :].rearrange("p (j r d) -> p j r d", r=2, d=32)
        cb_ = cosT[:L, pidx, :].unsqueeze(1).to_broadcast([L, 8, 32])
        sb_ = sinT[:L, pidx, :].unsqueeze(1).to_broadcast([L, 8, 32])
        r4 = rot[:L, :, :].rearrange("p j (r d) -> p j r d", r=2)
        tt("dve", rt1[:L], x4[:, :, 0, :], cb_, ALU.mult, (b0, cosT), (rt1,))
        tt("dve", rt2[:L], x4[:, :, 1, :], sb_, ALU.mult, (b0, sinT), (rt2,))
        tt("dve", r4[:, :, 0, :], rt1[:L], rt2[:L], ALU.subtract, (rt1, rt2), (rot,))
        tt("dve", rt1[:L], x4[:, :, 0, :], sb_, ALU.mult, (b0, sinT), (rt1,))
        tt("dve", rt2[:L], x4[:, :, 1, :], cb_, ALU.mult, (b0, cosT), (rt2,))
        tt("dve", r4[:, :, 1, :], rt1[:L], rt2[:L], ALU.add, (rt1, rt2), (rot,))
        qb_ = qk_b[alt()]
        tt("dve", qb_[:L, :].rearrange("p (j d) -> p j d", d=64), rot[:L], gqk[:L, 0:8].unsqueeze(2).to_broadcast([L, 8, 64]),
           ALU.mult, (rot, gqk), (qb_,))
        bkt = banks[2]; v2 = bfv(bkt)
        for j in range(8):
            tr(v2[0:64, j * 128:j * 128 + L], qb_[:L, j * 64:(j + 1) * 64], ident[:L, :L], (qb_, ident), (bkt,))
        qT = qkT[alt()]
        E("act", lambda: nc.scalar.copy(out=qT[:, :, 0:L], in_=v2[0:64, 0:1024].rearrange("p (j t) -> p j t", t=128)[:, :, 0:L]),
          (bkt,), (qT,))
        bs = banks[3]
        for h in range(4):
            mm(bs[:L, h * 128:h * 128 + L], qT[:, 4 + h, 0:L], qT[:, h, 0:L], True, True, (qT,), (bs,))
        pt = PT[alt()]
        tt("dve", pt[:L, :, 0:L], bs[:L, :].rearrange("p (h t) -> p h t", t=128)[:, :, 0:L],
           tri_f[:L, 0:L].unsqueeze(1).to_broadcast([L, 4, L]), ALU.mult, (bs, tri_f), (pt,))
        project(c, wi, 512, 512, banks[1])
        vb = v_b[alt()]
        E("act", lambda b_=banks[1]: nc.scalar.copy(out=vb[:L, :, 0:128], in_=b_[:L, :].rearrange("p (h v) -> p h v", v=128)),
          (banks[1],), (vb,))
        bo = banks[4]
        for h in range(4):
            mm(bo[:L, h * 128:(h + 1) * 128], pt[:L, h, 0:L], vb[:L, h, 0:128], True, False, (pt, vb), (bo,))
            mm(bo[:L, h * 128:(h + 1) * 128], qT[:, h, 0:L], Rbf[:, h, :], False, True, (qT, Rbf), (bo,))
        bu = banks[6]
        for h in range(4):
            mm(bu[0:64, h * 128:(h + 1) * 128], qb_[:L, 256 + h * 64:256 + (h + 1) * 64], vb[:L, h, 0:128], True, True,
               (qb_, vb), (bu,))
        Rs = Rst[l]
        gcol = 0 if L == 128 else 4
        tt("dve", Rs[:], Rs[:], bu[0:64, :].rearrange("p (h v) -> p h v", v=128), ALU.add, (Rs, bu), (Rs,))
        tt("dve", Rs[:], Rs[:], gl[:, gcol:gcol + 4].unsqueeze(2).to_broadcast([64, 4, 128]), ALU.mult, (Rs, gl), (Rs,))
        E("act", lambda: nc.scalar.copy(out=Rbf[:], in_=Rs[:]), (Rs,), (Rbf,))
        s2 = smsets[alt()][2]; s3 = smsets[alt()][3]
        o3 = bo[:L, :].rearrange("p (h v) -> p h v", v=128)
        E("dve", lambda: nc.vector.tensor_reduce(out=s2[:L, 0:4], in_=o3, axis=AX.X, op=ALU.add), (bo,), (s2,))
        ts("dve", s2[:L, 4:8], s2[:L, 0:4], -1.0 / 128, ALU.mult, (s2,), (s2,))
        gg = g1[alt()]
        for h in range(4):
            act(gg[:L, h * 128:(h + 1) * 128], bo[:L, h * 128:(h + 1) * 128], AF.Identity, (bo, s2), (gg,), bias=s2[:L, 4 + h:5 + h])
            act(junk[:L, 0:128], gg[:L, h * 128:(h + 1) * 128], AF.Square, (gg,), (junk, s3), accum=s3[:L, 8 + h:9 + h])
        act(s2[:L, 8:12], s3[:L, 8:12], AF.Ln, (s3,), (s2,), bias=EPS, scale=1.0 / 128)
        act(s2[:L, 12:16], s2[:L, 8:12], AF.Exp, (s2,), (s2,), scale=-0.5)
        project(c, wi, 1024, 512, banks[5])
        act(e1[:L, 0:512], banks[5][:L, :], AF.Exp, (banks[5],), (e1,), scale=-1.0)
        act(e1[:L, 0:512], e1[:L, 0:512], AF.Ln, (e1,), (e1,), bias=1.0)
        act(e2[:L, :], e1[:L, 0:512], AF.Exp, (e1,), (e2,), scale=-1.0)
        tt("dve", e2[:L, :], e2[:L, :], banks[5][:L, :], ALU.mult, (e2, banks[5]), (e2,))
        tt("dve", gg[:L, :].rearrange("p (h v) -> p h v", v=128), gg[:L, :].rearrange("p (h v) -> p h v", v=128),
           s2[:L, 12:16].unsqueeze(2).to_broadcast([L, 4, 128]), ALU.mult, (gg, s2), (gg,))
        omt = om[alt()]
        tt("dve", omt[:L, :], gg[:L, :], e2[:L, :], ALU.mult, (gg, e2), (omt,))
        out_proj(c, wi, omt, (7, 5))

    def ret_init(c, l):
        Rs = Rst[l]
        if c["seq"] < NSEQ:
            memset("pool", Rs, Rs[:], 0.0)
            memset("pool", Rbf, Rbf[:], 0.0)
        else:
            dma("sp", Rs[:], st_r[l].rearrange("h k v -> k h v"), W=(Rs,))
            E("act", lambda: nc.scalar.copy(out=Rbf[:], in_=Rs[:]), (Rs,), (Rbf,))

    def ret_final(c, l):
        dma("sp", o_r[l, c["seq"]].rearrange("h k v -> k h v"), Rst[l][:], R=(Rst[l],))

    def ssd_phase_setup(l, wi):
        a = aux[wi]
        for t8 in range(8):
            for w in range(4):
                idx = t8 * 4 + w
                ts("dve", cdiag[:, idx, :], ident_f[:, :], cwall[:, l, t8, w:w + 1], ALU.mult, (ident_f, cwall), (cdiag,))
        sa = saux[wi]
        act(sa[:, 24:32], sa[:, 8:16], AF.Exp, (sa,), (sa,))
        ts("dve", sa[:, 24:32], sa[:, 24:32], -1.0, ALU.mult, (sa,), (sa,))

    def ssd_chunk(c, l, wi):
        L = c["L"]; a = aux[wi]; xc = xcT[l]
        s0, s1, s2, s3 = smsets[alt()]
        project(c, wi, 512, 512, banks[0])
        E("act", lambda b_=banks[0]: nc.scalar.copy(out=xbc_b[:L, 0:512], in_=b_[:L, :]), (banks[0],), (xbc_b,))
        project(c, wi, 1024, 512, banks[1])
        E("act", lambda b_=banks[1]: nc.scalar.copy(out=xbc_b[:L, 512:1024], in_=b_[:L, :]), (banks[1],), (xbc_b,))
        if c["last"]:
            dma("pool", o_cv[l, c["seq"]], xbc_b[L - 3:L, :], R=(xbc_b,))
        bkt = banks[2]; v2 = bfv(bkt)
        for k in range(8):
            tr(v2[:, k * 128:k * 128 + L], xbc_b[:L, k * 128:(k + 1) * 128], ident[:L, :L], (xbc_b, ident), (bkt,))
        E("act", lambda: nc.scalar.copy(out=xc[:, :, 3:3 + L], in_=v2[:, 0:1024].rearrange("p (k t) -> p k t", t=128)[:, :, 0:L]),
          (bkt,), (xc,))
        xbt = xb_tok[alt()]
        for t8 in range(8):
            bo = banks[4 + t8 // 4]; o0 = (t8 % 4) * 128
            for w in range(4):
                mm(bo[:L, o0:o0 + 128], xc[:, t8, w:w + L], cdiag[:, t8 * 4 + w, :], w == 0, w == 3, (xc, cdiag), (bo,))
        vcopy("dve", xc[:, :, 0:3], xc[:, :, L:L + 3], (xc,), (xc,))
        dflat = decT[:L, :, :].rearrange("p h t -> p (h t)")
        for hp in range(2):
            tt("dve", e1[:L, hp * 512:(hp + 1) * 512], banks[4 + hp][:L, :], a[:L, 512 + hp * 512:1024 + hp * 512], ALU.add,
               (banks[4 + hp], a), (e1,))
        act(dflat, e1[:L, :], AF.Exp, (e1,), (decT,), scale=-1.0)
        act(dflat, dflat, AF.Ln, (decT,), (decT,), bias=1.0)
        act(dflat, dflat, AF.Exp, (decT,), (decT,), scale=-1.0)
        tt("dve", xbt[:L, :], e1[:L, :], dflat, ALU.mult, (e1, decT), (xbt,))
        bkt2 = banks[2]; v3 = bfv(bkt2)
        for k in range(4):
            tr(v3[:, k * 128:k * 128 + L], xbt[:L, 512 + k * 128:512 + (k + 1) * 128], ident[:L, :L], (xbt, ident), (bkt2,))
        xa = xact[alt()]
        E("act", lambda: nc.scalar.copy(out=xa[:, 4:8, 0:L], in_=v3[:, 0:512].rearrange("p (k t) -> p k t", t=128)[:, :, 0:L]),
          (bkt2,), (xa,))
        project(c, wi, 1536, 8, banks[7])
        bk = banks[7]
        sa = saux[wi]
        tt("dve", s0[:L, 0:8], bk[:L, 0:8], sa[:L, 0:8], ALU.add, (bk, sa), (s0,))
        act(s0[:L, 8:16], s0[:L, 0:8], AF.Exp, (s0,), (s0,))
        act(s0[:L, 16:24], s0[:L, 8:16], AF.Ln, (s0,), (s0,), bias=1.0)
        tt("dve", s0[:L, 24:32], s0[:L, 16:24], sa[:L, 24:32], ALU.mult, (s0, sa), (s0,))
        lh = smb[alt()]
        vcopy("dve", lh[:L, 0:8], s0[:L, 24:32], (s0,), (lh,))
        tt("dve", lh[:L, 8:16], s0[:L, 24:32], lh[:L, 0:8], ALU.subtract, (s0, lh), (lh,))
        mm(bk[:L, 16:24], tri[:L, :L], lh[:L, 0:8], True, False, (tri, lh), (bk,))
        mm(bk[:L, 16:24], tri[:L, :L], lh[:L, 8:16], False, True, (tri, lh), (bk,))
        mm(bk[:, 32:40], ones[:L, :], lh[:L, 0:8], True, False, (ones, lh), (bk,))
        mm(bk[:, 32:40], ones[:L, :], lh[:L, 8:16], False, True, (ones, lh), (bk,))
        vcopy("dve", s1[:, 0:8], bk[:, 32:40], (bk,), (s1,))
        vcopy("dve", s1[:L, 8:16], bk[:L, 16:24], (bk,), (s1,))
        act(s1[:L, 16:24], s1[:L, 8:16], AF.Exp, (s1,), (s1,))
        tt("dve", s1[:L, 24:32], s1[:L, 0:8], s1[:L, 8:16], ALU.subtract, (s1,), (s1,))
        act(s1[:L, 32:40], s1[:L, 24:32], AF.Exp, (s1,), (s1,))
        act(s1[:, 40:48], s1[:, 0:8], AF.Exp, (s1,), (s1,))
        for part in range(2):
            tt("dve", larhs[:L, part, :, 0:L], tri_f[:L, 0:L].unsqueeze(1).to_broadcast([L, 8, L]),
               lh[:L, part * 8:(part + 1) * 8].unsqueeze(2).to_broadcast([L, 8, L]), ALU.mult, (tri_f, lh), (larhs,))
        for h in range(8):
            bo = banks[4 + h // 4]; o0 = (h % 4) * 128
            mm(bo[:L, o0:o0 + L], ust[:L, :L], larhs[:L, 0, h, 0:L], True, False, (ust, larhs), (bo,))
            mm(bo[:L, o0:o0 + L], ust[:L, :L], larhs[:L, 1, h, 0:L], False, True, (ust, larhs), (bo,))
        for hp in range(2):
            bo = banks[4 + hp]
            act(decT[:L, 4 * hp:4 * hp + 4, 0:L], bo[:L, :].rearrange("p (h t) -> p h t", t=128)[:, :, 0:L], AF.Exp, (bo,), (decT,))
        bs = banks[6]
        for g in range(2):
            mm(bs[:L, g * 128:g * 128 + L], xa[:, 4 + g, 0:L], xa[:, 6 + g, 0:L], True, True, (xa,), (bs,))
        tt("dve", cbm[:L, :, 0:L], bs[:L, 0:256].rearrange("p (g t) -> p g t", t=128)[:, :, 0:L],
           tri_f[:L, 0:L].unsqueeze(1).to_broadcast([L, 2, L]), ALU.mult, (bs, tri_f), (cbm,))
        sc = scT[alt()]
        for g in range(2):
            tt("dve", sc[:L, 4 * g:4 * g + 4, 0:L], decT[:L, 4 * g:4 * g + 4, 0:L],
               cbm[:L, g, 0:L].unsqueeze(1).to_broadcast([L, 4, L]), ALU.mult, (decT, cbm), (sc,))
        xd = xdt[alt()]; xw_ = xw[alt()]
        tt("dve", xd[:L, :].rearrange("p (h d) -> p h d", d=64), xbt[:L, 0:512].rearrange("p (h d) -> p h d", d=64),
           s0[:L, 16:24].unsqueeze(2).to_broadcast([L, 8, 64]), ALU.mult, (xbt, s0), (xd,))
        tt("dve", xw_[:L, :].rearrange("p (h d) -> p h d", d=64), xd[:L, :].rearrange("p (h d) -> p h d", d=64),
           s1[:L, 32:40].unsqueeze(2).to_broadcast([L, 8, 64]), ALU.mult, (xd, s1), (xw_,))
        b0 = banks[4]; b1 = banks[5]
        for h in range(8):
            mm(b0[:L, h * 64:(h + 1) * 64], sc[:L, h, 0:L], xd[:L, h * 64:(h + 1) * 64], True, True, (sc, xd), (b0,))
        for g in range(2):
            mm(b1[:L, g * 256:(g + 1) * 256], xa[:, 6 + g, 0:L], Hbf[:, 4 * g:4 * g + 4, :].rearrange("p h d -> p (h d)"), True, True, (xa, Hbf), (b1,))
        tt("dve", yin[:L, :].rearrange("p (h d) -> p h d", d=64), b1[:L, :].rearrange("p (h d) -> p h d", d=64),
           s1[:L, 16:24].unsqueeze(2).to_broadcast([L, 8, 64]), ALU.mult, (b1, s1), (yin,))
        tt("dve", yin[:L, :], yin[:L, :], b0[:L, :], ALU.add, (yin, b0), (yin,))
        tt("dve", e2[:L, :].rearrange("p (h d) -> p h d", d=64), xbt[:L, 0:512].rearrange("p (h d) -> p h d", d=64),
           sa[:L, 16:24].unsqueeze(2).to_broadcast([L, 8, 64]), ALU.mult, (xbt, sa), (e2,))
        tt("dve", yin[:L, :], yin[:L, :], e2[:L, :], ALU.add, (yin, e2), (yin,))
        bu = banks[3]
        for g in range(2):
            mm(bu[:, g * 256:(g + 1) * 256], xbt[:L, 512 + g * 128:512 + (g + 1) * 128], xw_[:L, g * 256:(g + 1) * 256],
               True, True, (xbt, xw_), (bu,))
        Hs = Hst[l]
        tt("dve", Hs[:], Hs[:], s1[:, 40:48].unsqueeze(2).to_broadcast([128, 8, 64]), ALU.mult, (Hs, s1), (Hs,))
        tt("dve", Hs[:], Hs[:], bu[:, :].rearrange("p (h d) -> p h d", d=64), ALU.add, (Hs, bu), (Hs,))
        E("act", lambda: nc.scalar.copy(out=Hbf[:], in_=Hs[:]), (Hs,), (Hbf,))
        project(c, wi, 0, 512, banks[6])
        act(e1[:L, 0:512], banks[6][:L, :], AF.Exp, (banks[6],), (e1,), scale=-1.0)
        act(e1[:L, 0:512], e1[:L, 0:512], AF.Ln, (e1,), (e1,), bias=1.0)
        act(e1[:L, 512:1024], e1[:L, 0:512], AF.Exp, (e1,), (e1,), scale=-1.0)
        tt("dve", e1[:L, 512:1024], e1[:L, 512:1024], banks[6][:L, :], ALU.mult, (e1, banks[6]), (e1,))
        tt("dve", yin[:L, :], yin[:L, :], e1[:L, 512:1024], ALU.mult, (yin, e1), (yin,))
        for g in range(2):
            act(junk[:L, 0:256], yin[:L, g * 256:(g + 1) * 256], AF.Square, (yin,), (junk, s3), accum=s3[:L, 16 + g:17 + g])
        act(s2[:L, 0:2], s3[:L, 16:18], AF.Ln, (s3,), (s2,), bias=EPS, scale=1.0 / 256)
        act(s2[:L, 2:4], s2[:L, 0:2], AF.Exp, (s2,), (s2,), scale=-0.5)
        omt = om[alt()]
        for g in range(2):
            stt("dve", omt[:L, g * 256:(g + 1) * 256], yin[:L, g * 256:(g + 1) * 256], s2[:L, 2 + g:3 + g],
                a[:L, g * 256:(g + 1) * 256], ALU.mult, ALU.mult, (yin, s2, a), (omt,))
        out_proj(c, wi, omt, (3, 6))

    def ssd_init(c, l):
        Hs = Hst[l]; xc = xcT[l]
        if c["seq"] < NSEQ:
            memset("pool", Hs, Hs[:], 0.0)
            memset("pool", Hbf, Hbf[:], 0.0)
            memset("dve", xc, xc[:, :, 0:3], 0.0)
        else:
            dma("sp", Hs[:], st_s[l].rearrange("h n p -> n h p"), W=(Hs,))
            E("act", lambda: nc.scalar.copy(out=Hbf[:], in_=Hs[:]), (Hs,), (Hbf,))
            for w_ in range(3):
                dma("sp", e2[:, w_ * 8:(w_ + 1) * 8].unsqueeze(2), st_cv[l, w_].rearrange("(a p o) -> p a o", p=128, o=1), W=(e2,), nonc=True)
            vcopy("dve", xc[:, :, 0:3], e2[:, 0:24].rearrange("p (w a) -> p a w", w=3), (e2,), (xc,))

    def ssd_final(c, l):
        dma("sp", o_s[l, c["seq"]].rearrange("h n p -> n h p"), Hst[l][:], R=(Hst[l],))

    def hgrn_phase_setup(l, wi):
        a = aux[wi]
        for i in range(4):
            memset("pool", ke[i], ke[i][:], 0.0)
        if l == 1:
            tt("dve", a[:, 512:1024], a[:, 1024:1536], a[:, 512:1024], ALU.subtract, (a,), (a,))
            act(a[:, 512:1024], a[:, 512:1024], AF.Exp, (a,), (a,))
            ts("dve", a[:, 512:1024], a[:, 512:1024], 1.0, ALU.add, (a,), (a,))
            E("dve", lambda: nc.vector.reciprocal(out=a[:, 512:1024], in_=a[:, 512:1024]), (a,), (a,))

    def hgrn_chunk(c, l, wi):
        L = c["L"]; a = aux[wi]
        NB = (L + 31) // 32
        blocks = [(i * 32, min(L, (i + 1) * 32)) for i in range(NB)]
        s0, s1, s2, s3 = smsets[alt()]
        project(c, wi, 512, 512, banks[1])
        act(e2[:L, :], banks[1][:L, :], AF.Exp, (banks[1],), (e2,))
        act(e2[:L, :], e2[:L, :], AF.Ln, (e2,), (e2,), bias=1.0)
        act(kf[:L, :], e2[:L, :], AF.Exp, (e2,), (kf,), scale=-1.0)
        if l == 1:
            tt("dve", kf[:L, :], kf[:L, :], a[:L, 512:1024], ALU.mult, (kf, a), (kf,))
        act(e2[:L, :], kf[:L, :], AF.Ln, (kf,), (e2,), bias=1.0, scale=-1.0)
        hilo(e2[:L, :], L, 512, lfh, (e2,))
        kb = kb_t
        vcopy("pool", kb[:L, :], kf[:L, :], (kf,), (kb,))
        project(c, wi, 0, 512, banks[0])
        qb_ = qk_b[alt()]
        E("act", lambda b_=banks[0]: nc.scalar.activation(out=qb_[:L, :], in_=b_[:L, :], func=AF.Copy, scale=128 ** -0.5),
          (banks[0],), (qb_,))
        bb = banks[4]
        for h in range(4):
            mm(bb[:, h * 128:h * 128 + L], lfh[:L, 0, h * 128:(h + 1) * 128], tri[:L, :L], True, False, (lfh, tri), (bb,))
            mm(bb[:, h * 128:h * 128 + L], lfh[:L, 1, h * 128:(h + 1) * 128], tri[:L, :L], False, True, (lfh, tri), (bb,))
        E("act", lambda: nc.scalar.copy(out=bT[:, :, 0:L], in_=bb[:, :].rearrange("p (h t) -> p h t", t=128)[:, :, 0:L]), (bb,), (bT,))
        br = banks[5]
        mm(br[:L, :], ust[:L, :L], lfh[:L, 0, :], True, False, (ust, lfh), (br,))
        mm(br[:L, :], ust[:L, :L], lfh[:L, 1, :], False, True, (ust, lfh), (br,))
        act(e1[:L, 0:512], br[:L, :], AF.Exp, (br,), (e1,))
        kd = kd_b[alt()]
        tt("dve", kd[:L, :], e1[:L, 0:512], kf[:L, :], ALU.mult, (e1, kf), (kd,))
        bkt = banks[2]; v2 = bfv(bkt)
        for h in range(4):
            tr(v2[:, h * 128:h * 128 + L], qb_[:L, h * 128:(h + 1) * 128], ident[:L, :L], (qb_, ident), (bkt,))
            tr(v2[:, 512 + h * 128:512 + h * 128 + L], kb[:L, h * 128:(h + 1) * 128], ident[:L, :L], (kb, ident), (bkt,))
        E("act", lambda: nc.scalar.copy(out=qT_f[:, :, 0:L], in_=v2[:, 0:512].rearrange("p (h t) -> p h t", t=128)[:, :, 0:L]), (bkt,), (qT_f,))
        E("act", lambda: nc.scalar.copy(out=kT_b[:, :, 0:L], in_=v2[:, 512:1024].rearrange("p (h t) -> p h t", t=128)[:, :, 0:L]), (bkt,), (kT_b,))
        memset("pool", beta, beta[:, :, 0:1], 0.0)
        for i in range(1, NB):
            vcopy("pool", beta[:, :, i:i + 1], bT[:, :, 32 * i - 1:32 * i], (bT,), (beta,))
        for i, (t0, t1) in enumerate(blocks):
            tt("dve", dq[:, :, t0:t1], bT[:, :, t0:t1], beta[:, :, i:i + 1].to_broadcast([128, 4, t1 - t0]), ALU.subtract,
               (bT, beta), (dq,))
        act(dq[:, :, 0:L], dq[:, :, 0:L], AF.Exp, (dq,), (dq,))
        qe_ = qe[alt()]; qb2 = qb[alt()]
        tt("dve", qe_[:, :, 0:L], dq[:, :, 0:L], qT_f[:, :, 0:L], ALU.mult, (dq, qT_f), (qe_,))
        act(dq[:, :, 0:L], bT[:, :, 0:L], AF.Exp, (bT, dq), (dq,))
        tt("dve", qb2[:, :, 0:L], dq[:, :, 0:L], qT_f[:, :, 0:L], ALU.mult, (dq, qT_f), (qb2,))
        for i, (t0, t1) in enumerate(blocks):
            tt("pool", dk[:, :, 0:t1], beta[:, :, i:i + 1].to_broadcast([128, 4, t1]), bT[:, :, 0:t1], ALU.subtract, (beta, bT), (dk,))
            act(dk[:, :, 0:t1], dk[:, :, 0:t1], AF.Exp, (dk,), (dk,))
            tt("dve", ke[i][:, :, 0:t1], dk[:, :, 0:t1], kT_b[:, :, 0:t1], ALU.mult, (dk, kT_b), (ke[i],))
        bs = banks[3]
        for h in range(4):
            for i, (t0, t1) in enumerate(blocks):
                mm(bs[:L, h * 128 + t0:h * 128 + t1], ke[i][:, h, 0:L], qe_[:, h, t0:t1], True, True, (ke[i], qe_), (bs,))
        pt = PT[alt()]
        tt("dve", pt[:L, :, 0:L], bs[:L, :].rearrange("p (h t) -> p h t", t=128)[:, :, 0:L],
           tri_f[:L, 0:L].unsqueeze(1).to_broadcast([L, 4, L]), ALU.mult, (bs, tri_f), (pt,))
        project(c, wi, 1024, 512, banks[0])
        vb = v_b[alt()]
        E("act", lambda b_=banks[0]: nc.scalar.copy(out=vb[:L, :, 0:128], in_=b_[:L, :].rearrange("p (h v) -> p h v", v=128)),
          (banks[0],), (vb,))
        bo = banks[6]
        for h in range(4):
            mm(bo[:L, h * 128:(h + 1) * 128], pt[:L, h, 0:L], vb[:L, h, 0:128], True, False, (pt, vb), (bo,))
            mm(bo[:L, h * 128:(h + 1) * 128], qb2[:, h, 0:L], Gbf[:, h, :], False, True, (qb2, Gbf), (bo,))
        bu = banks[7]
        for h in range(4):
            mm(bu[:, h * 128:(h + 1) * 128], kd[:L, h * 128:(h + 1) * 128], vb[:L, h, 0:128], True, True, (kd, vb), (bu,))
        Gs = Gst[l]
        act(beta[:, :, 0:1], bT[:, :, L - 1:L], AF.Exp, (bT, beta), (beta,))
        tt("dve", Gs[:], Gs[:], beta[:, :, 0:1].to_broadcast([128, 4, 128]), ALU.mult, (Gs, beta), (Gs,))
        tt("dve", Gs[:], Gs[:], bu[:, :].rearrange("p (h v) -> p h v", v=128), ALU.add, (Gs, bu), (Gs,))
        E("act", lambda: nc.scalar.copy(out=Gbf[:], in_=Gs[:]), (Gs,), (Gbf,))
        for h in range(4):
            act(junk[:L, 0:128], bo[:L, h * 128:(h + 1) * 128], AF.Square, (bo,), (junk, s3), accum=s3[:L, 24 + h:25 + h])
        act(s2[:L, 0:4], s3[:L, 24:28], AF.Ln, (s3,), (s2,), bias=EPS, scale=1.0 / 128)
        act(s2[:L, 4:8], s2[:L, 0:4], AF.Exp, (s2,), (s2,), scale=-0.5)
        gg = g1[alt()]
        for h in range(4):
            stt("dve", gg[:L, h * 128:(h + 1) * 128], bo[:L, h * 128:(h + 1) * 128], s2[:L, 4 + h:5 + h],
                a[:L, h * 128:(h + 1) * 128], ALU.mult, ALU.mult, (bo, s2, a), (gg,))
        project(c, wi, 1536, 512, banks[5])
        act(e1[:L, 0:512], banks[5][:L, :], AF.Exp, (banks[5],), (e1,), scale=-1.0)
        act(e1[:L, 0:512], e1[:L, 0:512], AF.Ln, (e1,), (e1,), bias=1.0)
        act(e1[:L, 512:1024], e1[:L, 0:512], AF.Exp, (e1,), (e1,), scale=-1.0)
        tt("dve", e1[:L, 512:1024], e1[:L, 512:1024], banks[5][:L, :], ALU.mult, (e1, banks[5]), (e1,))
        omt = om[alt()]
        tt("dve", omt[:L, :], gg[:L, :], e1[:L, 512:1024], ALU.mult, (gg, e1), (omt,))
        out_proj(c, wi, omt, (3, 7))

    def hgrn_init(c, l):
        Gs = Gst[l]
        if c["seq"] < NSEQ:
            memset("pool", Gs, Gs[:], 0.0)
            memset("pool", Gbf, Gbf[:], 0.0)
        else:
            dma("sp", Gs[:], st_h[l].rearrange("h k v -> k h v"), W=(Gs,))
            E("act", lambda: nc.scalar.copy(out=Gbf[:], in_=Gs[:]), (Gs,), (Gbf,))

    def hgrn_final(c, l):
        dma("sp", o_h[l, c["seq"]].rearrange("h k v -> k h v"), Gst[l][:], R=(Gst[l],))

    CHUNK = {"M": mlstm_chunk, "R": ret_chunk, "S": ssd_chunk, "H": hgrn_chunk}
    INIT = {"M": mlstm_init, "R": ret_init, "S": ssd_init, "H": hgrn_init}
    SHADOW = {"M": (Cst, Cbf), "R": (Rst, Rbf), "S": (Hst, Hbf), "H": (Gst, Gbf)}

    phases = []
    for ui, unit in enumerate(units):
        for l in range(2):
            for m in MIX:
                if m in enabled:
                    phases.append((ui, l, m))
    wi_of = {}
    wi_of[0] = load_weights(phases[0][1], phases[0][2])
    last_shadow = {m: None for m in MIX}
    seq_chunks_done = {}
    for pi, (ui, l, m) in enumerate(phases):
        unit = units[ui]
        first_phase_of_layer = (m == [mm_ for mm_ in MIX if mm_ in enabled][0])
        if first_phase_of_layer:
            P.barrier()
            if l == 0:
                for c in unit:
                    src = xs if c["seq"] == NSEQ else xp[c["seq"], c["ci"] * 128:(c["ci"] + 1) * 128, :]
                    dma("sp", x_t[c["slot"]][:c["L"], :], src, W=(x_t[c["slot"]],))
            dma("sp", normw[:], w_norm[l:l + 1, :].partition_broadcast(128), W=(normw,))
            for c in unit:
                layer_norm_transpose(c)
        if pi + 1 < len(phases):
            wi_of[pi + 1] = load_weights(phases[pi + 1][1], phases[pi + 1][2])
        wi = wi_of[pi]
        P.barrier()
        if m == "S":
            ssd_phase_setup(l, wi)
        if m == "H":
            hgrn_phase_setup(l, wi)
        for c in unit:
            key = (c["seq"], l, m)
            if c["first"]:
                P.window()
                INIT[m](c, l)
            elif last_shadow[m] != (c["seq"], l):
                stt_, bf_ = SHADOW[m]
                E("act", lambda s_=stt_[l], b_=bf_: nc.scalar.copy(out=b_[:], in_=s_[:]), (stt_[l],), (bf_,))
            last_shadow[m] = (c["seq"], l)
            next_chunk()
            if m == "M":
                P.window(sched=(M_PIN is not None), pin=(M_PIN or ()))
            else:
                P.window()
            CHUNK[m](c, l, wi)
            if c["last"]:
                P.window()
                if m == "M":
                    mlstm_final(c, l, 1 if c["seq"] == NSEQ else NCT, NCT if c["seq"] == NSEQ else 0)
                elif m == "R":
                    ret_final(c, l)
                elif m == "S":
                    ssd_final(c, l)
                else:
                    hgrn_final(c, l)
        last_mixer = [mm_ for mm_ in MIX if mm_ in enabled][-1]
        if l == 1 and m == last_mixer:
            P.barrier()
            dma("sp", normw[:], w_fn[0:1, :].partition_broadcast(128), W=(normw,))
            for c in unit:
                L = c["L"]; xt = x_t[c["slot"]]
                s = sm[0]
                act(junk[:L, :], xt[:L, :], AF.Square, (xt,), (junk, s), accum=s[:L, 0:1])
                act(s[:L, 1:2], s[:L, 0:1], AF.Ln, (s,), (s,), bias=EPS, scale=1.0 / D)
                act(s[:L, 2:3], s[:L, 1:2], AF.Exp, (s,), (s,), scale=-0.5)
                stt("dve", xt[:L, :], xt[:L, :], s[:L, 2:3], normw[:L, :], ALU.mult, ALU.mult, (xt, s, normw), (xt,))
                dst = y_s if c["seq"] == NSEQ else y_p[c["seq"], c["ci"] * 128:(c["ci"] + 1) * 128, :]
                dma("sp", dst, xt[:L, :], R=(xt,))

    P.barrier()
    fin = P.emit("sp", lambda: None)

    def fin_cb():
        for e in ("sp", "pool", "act"):
            for op in P.ops[e]:
                if op.dma:
                    fin.deps.append(op)

    P.finalize(st, fin_cb)
    st.close()
    return nc


def host_consts(NCT, TS=16, PAST=1024):
    ident = np.eye(128, dtype=np.float32)
    r = np.arange(128)
    tri = (r[:, None] <= r[None, :]).astype(np.float32)
    ust = (r[:, None] > r[None, :]).astype(np.float32)
    half = 32
    freqs = (10000.0 ** (-np.arange(half, dtype=np.float32) / half)).astype(np.float32)
    cos = np.zeros((128, NCT + 1, 32), np.float32); sin = np.zeros((128, NCT + 1, 32), np.float32)
    for ci in range(NCT):
        pos = (ci * 128 + r).astype(np.float32)
        ang = pos[:, None] * freqs[None, :]
        cos[:, ci] = np.cos(ang); sin[:, ci] = np.sin(ang)
    pos = (PAST + np.arange(TS)).astype(np.float32)
    ang = pos[:, None] * freqs[None, :]
    cos[:TS, NCT] = np.cos(ang); sin[:TS, NCT] = np.sin(ang)
    lg = np.log1p(-(2.0 ** (-5.0 - np.arange(4, dtype=np.float64))))
    gqk = np.zeros((128, 8), np.float32)
    gqk[:, 0:4] = np.exp((r[:, None] + 1.0) * lg[None, :]) * (64 ** -0.5)
    gqk[:, 4:8] = np.exp(-(r[:, None] + 1.0) * lg[None, :])
    gl = np.zeros((64, 8), np.float32)
    gl[:, 0:4] = np.exp(128.0 * lg)[None, :]
    gl[:, 4:8] = np.exp(float(TS) * lg)[None, :]
    return dict(c_id=ident, c_tri=tri, c_ust=ust, c_cos=cos, c_sin=sin, c_gqk=gqk, c_gl=gl)


_CACHE = {}


def run_cores(per_core_inputs, NSEQ, TP, NCH, enabled=MIX, core_ids=None):
    key = (NSEQ, TP, NCH, tuple(enabled))
    if key not in _CACHE:
        _CACHE[key] = build_program(NSEQ, TP, NCH, enabled=enabled)
    nc = _CACHE[key]
    if core_ids is None:
        core_ids = list(range(len(per_core_inputs)))
    res = run_bass_kernel_spmd(nc, per_core_inputs, core_ids=core_ids)
    return res.results


def make_core_inputs(xp, xs, states, weights, NCT):
    f = lambda a: np.ascontiguousarray(a, dtype=np.float32)
    m = dict(xp=f(xp), xs=f(xs))
    m.update({k: f(v) for k, v in states.items()})
    m.update({k: f(v) for k, v in weights.items()})
    m.update(host_consts(NCT))
    return m


def kernel(x_prompt, x_sample, state_mlstm_c, state_mlstm_n, state_mlstm_m, state_ret, state_ssd,
           cache_ssd_conv, state_hgrn, norm_w, w_in, mlstm_gate_b, mlstm_norm_w, ssd_conv_w, ssd_conv_b,
           ssd_dt_bias, ssd_a_log, ssd_d, ssd_norm_w, hgrn_lower_bounds, hgrn_norm_w, w_out, final_norm_w):
    x_prompt = np.asarray(x_prompt); x_sample = np.asarray(x_sample)
    B, TP, _ = x_prompt.shape
    NCORES = 8
    NSEQ = B // NCORES
    NCT = TP // 128
    weights = dict(w_norm=norm_w, w_in=w_in, w_gb=mlstm_gate_b, w_mnw=mlstm_norm_w, w_cw=ssd_conv_w, w_cb=ssd_conv_b,
                   w_dtb=ssd_dt_bias, w_alog=ssd_a_log, w_sd=ssd_d, w_snw=ssd_norm_w, w_lb=hgrn_lower_bounds,
                   w_hnw=hgrn_norm_w, w_out=w_out, w_fn=np.asarray(final_norm_w).reshape(1, -1))
    weights = {k: np.asarray(v) for k, v in weights.items()}
    ins = []
    for cidx in range(NCORES):
        states = dict(st_mc=np.asarray(state_mlstm_c)[:, cidx], st_mn=np.asarray(state_mlstm_n)[:, cidx],
                      st_mm=np.asarray(state_mlstm_m)[:, cidx], st_r=np.asarray(state_ret)[:, cidx],
                      st_s=np.asarray(state_ssd)[:, cidx], st_cv=np.asarray(cache_ssd_conv)[:, cidx],
                      st_h=np.asarray(state_hgrn)[:, cidx])
        ins.append(make_core_inputs(x_prompt[cidx * NSEQ:(cidx + 1) * NSEQ], x_sample[cidx], states, weights, NCT))
    res = run_cores(ins, NSEQ, TP, NCH=4)
    cat = lambda k, sl: np.concatenate([r[k][sl] for r in res], axis=0)
    y_prompt = np.concatenate([r["y_p"] for r in res], axis=0)
    y_sample = np.stack([r["y_s"] for r in res], axis=0)

    def pst(k):
        return np.concatenate([r[k][:, 0:NSEQ] for r in res], axis=1)

    def sst(k):
        return np.concatenate([r[k][:, NSEQ:NSEQ + 1] for r in res], axis=1)

    names = ["o_mc", "o_mn", "o_mm", "o_r", "o_s", "o_cv", "o_h"]
    outs = [y_prompt, y_sample] + [pst(k) for k in names] + [sst(k) for k in names]
    return tuple(np.ascontiguousarray(o, dtype=np.float32) for o in outs)
```

```python
import math
from contextlib import ExitStack
import numpy as np
import concourse.bass as bass
import concourse.mybir as mybir
from concourse.bass_utils import run_bass_kernel_spmd

F32 = mybir.dt.float32
BF16 = mybir.dt.bfloat16
AF = mybir.ActivationFunctionType
ALU = mybir.AluOpType
AX = mybir.AxisListType

D = 1024
EPS = 1e-6
INC = 7184
ENGS = ("pe", "act", "dve", "pool", "sp")


class Reg:
    __slots__ = ("w", "r")

    def __init__(self):
        self.w = None
        self.r = []


class Op:
    __slots__ = ("eng", "fn", "deps", "dma", "signal", "sem", "semval", "prec", "seg", "win", "gi", "prio", "succ", "cost")

    def __init__(self, eng, fn, dma):
        self.eng = eng
        self.fn = fn
        self.deps = []
        self.dma = dma
        self.signal = False
        self.sem = None
        self.semval = 0
        self.prec = []
        self.seg = 0
        self.win = 0
        self.gi = 0
        self.prio = 0.0
        self.succ = []
        self.cost = 0.5


class T:
    def __init__(self, t):
        self.t = t
        self.reg = Reg()
        self.arena = False

    def __getitem__(self, k):
        return self.t[k]


COST = {"pe": 0.25, "act": 0.5, "dve": 0.6, "pool": 1.0, "sp": 0.1}
M_PIN = ("act", "pool")
SCHED_MODE = "segment"


class Prog:
    def __init__(self, nc, n_dma_sems=8):
        self.nc = nc
        self.ops = {e: [] for e in ENGS}
        self.n_dma_sems = n_dma_sems
        self.all = []
        self.seg = 0
        self.win = 0
        self.nosched = set()
        self.pins = {}

    def barrier(self):
        self.seg += 1
        self.win += 1

    def window(self, sched=True, pin=()):
        self.win += 1
        if not sched:
            self.nosched.add(self.win)
        if pin:
            self.pins[self.win] = tuple(pin)

    def emit(self, eng, fn, R=(), W=(), dma=False, cost=None):
        op = Op(eng, fn, dma)
        op.seg = self.seg
        op.win = self.win
        op.gi = len(self.all)
        op.cost = (2.0 if dma else COST[eng]) if cost is None else cost
        if dma and any(t_.arena for t_ in tuple(R) + tuple(W)):
            op.signal = True
            op.prio = -1.0
        deps = []
        for r in R:
            r = r.reg
            if r.w is not None:
                deps.append(r.w)
        for w in W:
            w = w.reg
            if w.w is not None:
                deps.append(w.w)
            deps.extend(w.r)
        seen = set()
        for d in deps:
            if d is op or id(d) in seen:
                continue
            seen.add(id(d))
            op.prec.append(d)
        for r in R:
            r.reg.r.append(op)
        for w in W:
            w.reg.w = op
            w.reg.r = []
        self.ops[eng].append(op)
        self.all.append(op)
        return op

    @staticmethod
    def _list_schedule(ops, pin=()):
        inwin = set(id(o) for o in ops)
        for o in ops:
            o.succ = []
        indeg = {}
        lastdma = {}
        for o in ops:
            n = 0
            for d in o.prec:
                if id(d) in inwin:
                    d.succ.append(o)
                    n += 1
            if o.dma or o.eng in pin:
                pd = lastdma.get(o.eng)
                if pd is not None and pd not in o.prec:
                    pd.succ.append(o)
                    n += 1
                lastdma[o.eng] = o
            indeg[id(o)] = n
        prio = {}
        for o in reversed(ops):
            best = 0.0
            for sc in o.succ:
                p_ = prio[id(sc)]
                if p_ > best:
                    best = p_
            prio[id(o)] = best + o.cost
        ready = [o for o in ops if indeg[id(o)] == 0]
        rtime = {id(o): 0.0 for o in ready}
        efree = {e: 0.0 for e in ENGS}
        out = []
        nleft = len(ops)
        while nleft:
            best = None
            bkey = None
            for o in ready:
                st_ = rtime[id(o)]
                ef = efree[o.eng]
                if ef > st_:
                    st_ = ef
                key = (st_, -prio[id(o)], o.gi)
                if bkey is None or key < bkey:
                    bkey = key
                    best = o
            o = best
            ready.remove(o)
            st_ = bkey[0]
            if o.dma:
                efree[o.eng] = st_ + 0.1
                f = st_ + o.cost
            else:
                f = st_ + o.cost
                efree[o.eng] = f
            out.append(o)
            nleft -= 1
            for sc in o.succ:
                indeg[id(sc)] -= 1
                if f > rtime.get(id(sc), 0.0):
                    rtime[id(sc)] = f
                if indeg[id(sc)] == 0:
                    ready.append(sc)
                    rtime.setdefault(id(sc), 0.0)
        return out

    def schedule(self):
        arena_flag = {id(o): (o.dma and o.prio == -1.0) for o in self.all}
        groups = []
        key = None
        for o in self.all:
            k = (o.seg, o.win if SCHED_MODE == "window" else 0)
            if k != key:
                groups.append([])
                key = k
            groups[-1].append(o)
        new_all = []
        for g in groups:
            if SCHED_MODE == "none" or len(g) < 3 or any(o_.win in self.nosched for o_ in g):
                new_all.extend(g)
            else:
                pin_ = ()
                for o_ in g:
                    if o_.win in self.pins:
                        pin_ = self.pins[o_.win]
                        break
                new_all.extend(self._list_schedule(g, pin_))
        new_ops = {e: [] for e in ENGS}
        for o in new_all:
            new_ops[o.eng].append(o)
        nseg = self.seg + 1
        last_compute = {}
        seg_first = {}
        seg_last = {}
        seg_arena = {}
        for e in ENGS:
            for o in new_ops[e]:
                if (o.seg, e) not in seg_first:
                    seg_first[(o.seg, e)] = o
                if not o.dma:
                    seg_last[(o.seg, e)] = o
                elif arena_flag[id(o)]:
                    seg_arena.setdefault(o.seg, []).append(o)
        carry = {}
        for k in range(nseg):
            fence = list(carry.values()) + seg_arena.get(k - 1, [])
            if k > 0:
                for e in ENGS:
                    fo = seg_first.get((k, e))
                    if fo is not None:
                        fo.prec = list(fo.prec) + [d for d in fence if d is not fo]
            for e in ("pe", "act", "dve", "pool"):
                if (k, e) in seg_last:
                    carry[e] = seg_last[(k, e)]
        self.ops = new_ops
        for e in ENGS:
            for op in self.ops[e]:
                op.deps = []
                op.signal = False
        pos = {}
        for e in ENGS:
            for k_, op in enumerate(self.ops[e]):
                pos[id(op)] = k_
        for e in ENGS:
            for op in self.ops[e]:
                seen = set()
                latest = {}
                for d in op.prec:
                    if id(d) in seen or d is op:
                        continue
                    seen.add(id(d))
                    if d.dma:
                        op.deps.append(d)
                        continue
                    if d.eng == op.eng and op.eng == "pe":
                        continue
                    cur = latest.get(d.eng)
                    if cur is None or pos[id(d)] > pos[id(cur)]:
                        latest[d.eng] = d
                for d in latest.values():
                    op.deps.append(d)
                    d.signal = True

    def finalize(self, stack, fin_cb=None):
        nc = self.nc
        self.schedule()
        if fin_cb is not None:
            fin_cb()
        engmap = {"pe": nc.tensor, "act": nc.scalar, "dve": nc.vector, "pool": nc.gpsimd, "sp": nc.sync}
        esem = {e: stack.enter_context(nc.semaphore("es_" + e)) for e in ENGS}
        dsem = {e: [stack.enter_context(nc.semaphore("ds_%s_%d" % (e, i))) for i in range(self.n_dma_sems)]
                for e in ("sp", "pool", "act")}
        dcount = {e: [0] * self.n_dma_sems for e in dsem}
        for e in ENGS:
            c = 0
            k = 0
            for op in self.ops[e]:
                if op.dma:
                    j = k % self.n_dma_sems
                    k += 1
                    dcount[e][j] += 16
                    op.sem = dsem[e][j]
                    op.semval = dcount[e][j]
                elif op.signal:
                    c += 1
                    op.sem = esem[e]
                    op.semval = c
        block = stack.enter_context(nc.Block())

        def run(e):
            engine = engmap[e]
            known = {}
            for op in self.ops[e]:
                waits = {}
                for d in op.deps:
                    key = id(d.sem)
                    if known.get(key, 0) >= d.semval:
                        continue
                    if key not in waits or waits[key][1] < d.semval:
                        waits[key] = (d.sem, d.semval)
                if op.dma and op.semval > 16:
                    key = id(op.sem)
                    v = op.semval - 16
                    if known.get(key, 0) < v and (key not in waits or waits[key][1] < v):
                        waits[key] = (op.sem, v)
                for key, (s, v) in waits.items():
                    engine.wait_ge(s, v)
                    known[key] = v
                ins = op.fn()
                if ins is None:
                    continue
                if op.dma:
                    ins.then_inc(op.sem, 16)
                elif op.signal:
                    ins.then_inc(op.sem, 1)

        @block.tensor
        def _(t):
            run("pe")

        @block.scalar
        def _(t):
            run("act")

        @block.vector
        def _(t):
            run("dve")

        @block.gpsimd
        def _(t):
            run("pool")

        @block.sync
        def _(t):
            run("sp")


MIX = ("M", "R", "S", "H")
WCOLS = {"M": (0, 2056), "R": (2056, 1536), "S": (3592, 1544), "H": (5136, 2048)}
GROUPS = {
    "M": [(0, 512), (512, 512), (1024, 512), (1536, 512), (2048, 8)],
    "R": [(0, 512), (512, 512), (1024, 512)],
    "S": [(0, 512), (512, 512), (1024, 512), (1536, 8)],
    "H": [(0, 512), (512, 512), (1024, 512), (1536, 512)],
}


def build_program(NSEQ, TP, NCH, TS=16, PAST=1024, enabled=MIX):
    nc = bass.Bass("TRN2", target_bir_lowering=False)
    st = ExitStack()
    P = Prog(nc)

    def din(name, shape):
        return nc.dram_tensor(name, list(shape), F32, kind="ExternalInput").ap()

    def dout(name, shape):
        return nc.dram_tensor(name, list(shape), F32, kind="ExternalOutput").ap()

    NCT = TP // 128
    xp = din("xp", [NSEQ, TP, D])
    xs = din("xs", [TS, D])
    st_mc = din("st_mc", [2, 4, 64, 128]); st_mn = din("st_mn", [2, 4, 64]); st_mm = din("st_mm", [2, 4])
    st_r = din("st_r", [2, 4, 64, 128]); st_s = din("st_s", [2, 8, 128, 64])
    st_cv = din("st_cv", [2, 3, 1024]); st_h = din("st_h", [2, 4, 128, 128])
    w_norm = din("w_norm", [2, D]); w_in = din("w_in", [2, D, INC]); w_gb = din("w_gb", [2, 8])
    w_mnw = din("w_mnw", [2, 512]); w_cw = din("w_cw", [2, 4, 1024]); w_cb = din("w_cb", [2, 1024])
    w_dtb = din("w_dtb", [2, 8]); w_alog = din("w_alog", [2, 8]); w_sd = din("w_sd", [2, 8])
    w_snw = din("w_snw", [2, 512]); w_lb = din("w_lb", [2, 512]); w_hnw = din("w_hnw", [2, 512])
    w_out = din("w_out", [2, 2048, D]); w_fn = din("w_fn", [1, D])
    c_id = din("c_id", [128, 128]); c_tri = din("c_tri", [128, 128]); c_ust = din("c_ust", [128, 128])
    c_cos = din("c_cos", [128, NCT + 1, 32]); c_sin = din("c_sin", [128, NCT + 1, 32])
    c_gqk = din("c_gqk", [128, 8]); c_gl = din("c_gl", [64, 8])

    y_p = dout("y_p", [NSEQ, TP, D]); y_s = dout("y_s", [TS, D])
    NS1 = NSEQ + 1
    o_mc = dout("o_mc", [2, NS1, 4, 64, 128]); o_mn = dout("o_mn", [2, NS1, 4, 64]); o_mm = dout("o_mm", [2, NS1, 4])
    o_r = dout("o_r", [2, NS1, 4, 64, 128]); o_s = dout("o_s", [2, NS1, 8, 128, 64])
    o_cv = dout("o_cv", [2, NS1, 3, 1024]); o_h = dout("o_h", [2, NS1, 4, 128, 128])
    scr_m = nc.dram_tensor("scr_m", [2, NS1, 4], F32, kind="Internal").ap()

    def sb(name, shape, dt=F32):
        return T(st.enter_context(nc.sbuf_tensor(name, list(shape), dt)))

    phys_banks = [T(st.enter_context(nc.psum_tensor("bank%d" % i, [128, 512], F32))) for i in range(8)]
    tick = {"n": 0}
    BANK_ROT = 0

    class BankMap:
        def __getitem__(self, k):
            return phys_banks[(k + BANK_ROT * (tick["n"] % 2)) % 8]

    banks = BankMap()

    def bfv(bank):
        return bank.t[:].bitcast(BF16)

    def E(eng, fn, R=(), W=()):
        return P.emit(eng, fn, R, W)

    def dma(q, out, in_, R=(), W=(), nonc=False):
        engine = {"sp": nc.sync, "pool": nc.gpsimd, "act": nc.scalar}[q]
        if nonc:
            return P.emit(q, lambda: engine.dma_start(out=out, in_=in_, allow_slow_non_contiguous=True), R, W, dma=True)
        return P.emit(q, lambda: engine.dma_start(out=out, in_=in_), R, W, dma=True)

    def mm(out, lhsT, rhs, start, stop, R, W):
        return E("pe", lambda: nc.tensor.matmul(out, lhsT=lhsT, rhs=rhs, start=start, stop=stop), R, W)

    def tr(out, in_, ident, R, W):
        return E("pe", lambda: nc.tensor.transpose(out, in_, ident), R, W)

    def act(out, in_, func, R, W, bias=None, scale=None, accum=None):
        kw = {}
        if bias is not None:
            kw["bias"] = bias
        if scale is not None:
            kw["scale"] = scale
        if accum is not None:
            kw["accum_out"] = accum
        return E("act", lambda: nc.scalar.activation(out=out, in_=in_, func=func, **kw), R, W)

    def vcopy(eng, out, in_, R, W):
        e = {"dve": nc.vector, "pool": nc.gpsimd}[eng]
        return E(eng, lambda: e.tensor_copy(out=out, in_=in_), R, W)

    def tt(eng, out, a, b, op, R, W):
        e = {"dve": nc.vector, "pool": nc.gpsimd}[eng]
        return E(eng, lambda: e.tensor_tensor(out=out, in0=a, in1=b, op=op), R, W)

    def ts(eng, out, a, s1, op0, R, W, s2=None, op1=None):
        e = {"dve": nc.vector, "pool": nc.gpsimd}[eng]
        if op1 is None:
            return E(eng, lambda: e.tensor_scalar(out=out, in0=a, scalar1=s1, scalar2=None, op0=op0), R, W)
        return E(eng, lambda: e.tensor_scalar(out=out, in0=a, scalar1=s1, scalar2=s2, op0=op0, op1=op1), R, W)

    def stt(eng, out, a, s, b, op0, op1, R, W):
        e = {"dve": nc.vector, "pool": nc.gpsimd}[eng]
        return E(eng, lambda: e.scalar_tensor_tensor(out=out, in0=a, scalar=s, in1=b, op0=op0, op1=op1), R, W)

    def memset(eng, t_, ap, val):
        e = {"dve": nc.vector, "pool": nc.gpsimd}[eng]
        return E(eng, lambda: e.memset(ap, val), (), (t_,))

    ident_f = sb("ident_f", [128, 128]); ident = sb("ident", [128, 128], BF16)
    tri_f = sb("tri_f", [128, 128]); tri = sb("tri", [128, 128], BF16)
    ust = sb("ust", [128, 128], BF16); ust_f = sb("ust_f", [128, 128])
    ones = sb("ones", [128, 128], BF16)
    cosT = sb("cosT", [128, NCT + 1, 32]); sinT = sb("sinT", [128, NCT + 1, 32])
    gqk = sb("gqk", [128, 8]); gl = sb("gl", [64, 8])
    dma("sp", ident_f[:], c_id, W=(ident_f,)); dma("sp", tri_f[:], c_tri, W=(tri_f,)); dma("sp", ust_f[:], c_ust, W=(ust_f,))
    dma("sp", cosT[:], c_cos, W=(cosT,)); dma("sp", sinT[:], c_sin, W=(sinT,))
    dma("sp", gqk[:], c_gqk, W=(gqk,)); dma("sp", gl[:], c_gl, W=(gl,))
    vcopy("dve", ident[:], ident_f[:], (ident_f,), (ident,))
    vcopy("dve", tri[:], tri_f[:], (tri_f,), (tri,))
    vcopy("dve", ust[:], ust_f[:], (ust_f,), (ust,))
    memset("dve", ones, ones[:], 1.0)

    cwall = sb("cwall", [128, 2, 8, 4]); cball = sb("cball", [128, 2, 8]); cbneg = sb("cbneg", [128, 2, 8])
    for l_ in range(2):
        for w_ in range(4):
            dma("sp", cwall[:, l_, :, w_:w_ + 1], w_cw[l_, w_].rearrange("(a p o) -> p a o", p=128, o=1), W=(cwall,), nonc=True)
        dma("sp", cball[:, l_, :].unsqueeze(2), w_cb[l_].rearrange("(a p o) -> p a o", p=128, o=1), W=(cball,), nonc=True)
    ts("dve", cbneg[:], cball[:], -1.0, ALU.mult, (cball,), (cbneg,))

    NU = NCH
    x_t = [sb("x%d" % i, [128, D]) for i in range(NU)]
    hnT = sb("hnT", [128, 8, NU * 128], BF16)
    hnT_r = [T(None) for _ in range(NU)]
    wslot = [sb("wslot%d" % i, [128, 8, 2056], BF16) for i in range(2)]
    woslot = [sb("woslot%d" % i, [128, 4, D], BF16) for i in range(2)]
    aux = [sb("aux%d" % i, [128, 1536]) for i in range(2)]
    saux = [sb("saux%d" % i, [128, 32]) for i in range(2)]
    junk = sb("junk", [128, D], BF16)

    Cst = [sb("Cst%d" % l, [64, 4, 129]) for l in range(2)]; Cbf = sb("Cbf", [64, 4, 129], BF16)
    Rst = [sb("Rst%d" % l, [64, 4, 128]) for l in range(2)]; Rbf = sb("Rbf", [64, 4, 128], BF16)
    Hst = [sb("Hst%d" % l, [128, 8, 64]) for l in range(2)]; Hbf = sb("Hbf", [128, 8, 64], BF16)
    Gst = [sb("Gst%d" % l, [128, 4, 128]) for l in range(2)]; Gbf = sb("Gbf", [128, 4, 128], BF16)
    xcT = [sb("xcT%d" % l, [128, 8, 131], BF16) for l in range(2)]
    mG = [sb("mG%d" % l, [4, NCT + 1]) for l in range(2)]
    mB = [sb("mB%d" % l, [4, NCT + 1]) for l in range(2)]
    m0 = [sb("m0_%d" % l, [4, 1]) for l in range(2)]

    ARENA_B = 43072 + 5184
    arena = st.enter_context(nc.sbuf_tensor("arena", [128, ARENA_B // 2], BF16))

    class Lay:
        def __init__(self, base):
            self.off = base

        def take(self, shape, dt=F32):
            n = 1
            for d_ in shape[1:]:
                n *= d_
            nb = n * (4 if dt == F32 else 2)
            a_ = arena[0:shape[0], self.off // 2:(self.off + nb) // 2]
            if dt == F32:
                a_ = a_.bitcast(F32)
            if len(shape) == 3:
                a_ = a_.rearrange("p (a b) -> p a b", b=shape[2])
            elif len(shape) == 4:
                a_ = a_.rearrange("p (a b c) -> p a b c", b=shape[2], c=shape[3])
            self.off += (nb + 63) // 64 * 64
            assert self.off <= ARENA_B, (self.off, ARENA_B)
            t_ = T(a_)
            t_.arena = True
            return t_

        def take2(self, shape, dt=F32):
            t_ = self.take(shape, dt)
            return [t_, t_]

    smsets = [[sb("sm%d_%d" % (i, j), [128, 64]) for i in range(4)] for j in range(2)]
    sm = smsets[0]
    smb = [sb("smb%d" % j, [128, 64], BF16) for j in range(2)]
    tstat = sb("tstat", [8, 132])
    lc = Lay(0)
    qk_b = [lc.take([128, 512], BF16) for _ in range(2)]
    v_b = [lc.take([128, 4, 129], BF16) for _ in range(2)]
    e1 = lc.take([128, 1024]); e2 = lc.take([128, 512]); g1 = lc.take2([128, 512])
    PT = [lc.take([128, 4, 128], BF16) for _ in range(2)]
    om = [lc.take([128, 512], BF16) for _ in range(2)]
    omT = [lc.take([128, 4, 128], BF16) for _ in range(2)]
    base = lc.off
    ln_ = Lay(base)
    hn_b = ln_.take2([128, D], BF16)
    normw = ln_.take([128, D])
    lm = Lay(base)
    qkT = lm.take2([64, 8, 128], BF16)
    rot = lm.take([128, 8, 64]); rt1 = lm.take([128, 8, 32]); rt2 = lm.take([128, 8, 32])
    ls = Lay(base)
    xbc_b = ls.take([128, 1024], BF16)
    cdiag = ls.take([128, 32, 128], BF16)
    xact = ls.take2([128, 8, 128], BF16)
    xb_tok = ls.take2([128, 1024], BF16)
    larhs = ls.take([128, 2, 8, 128], BF16)
    decT = ls.take([128, 8, 128])
    cbm = ls.take([128, 2, 128])
    scT = ls.take2([128, 8, 128], BF16)
    xdt = ls.take2([128, 512], BF16); xw = ls.take2([128, 512], BF16)
    yin = ls.take([128, 512])
    lh_ = Lay(base)
    kf = lh_.take([128, 512]); lfh = lh_.take([128, 2, 512], BF16)
    bT = lh_.take([128, 4, 128]); beta = lh_.take([128, 4, 4])
    dq = lh_.take([128, 4, 128]); dk = lh_.take([128, 4, 128])
    kT_b = lh_.take([128, 4, 128], BF16); qT_f = lh_.take([128, 4, 128])
    qe = lh_.take2([128, 4, 128], BF16); qb = lh_.take2([128, 4, 128], BF16)
    ke = [lh_.take([128, 4, 128], BF16) for i in range(4)]
    kd_b = lh_.take2([128, 512], BF16)
    kb_t = lh_.take([128, 512], BF16)
    pass

    units = []
    for s in range(NSEQ):
        for u0 in range(0, NCT, NCH):
            ch = []
            for j in range(NCH):
                ci = u0 + j
                if ci >= NCT:
                    break
                ch.append(dict(seq=s, L=128, slot=j, ci=ci, first=(ci == 0), last=(ci == NCT - 1), pidx=ci))
            units.append(ch)
    units = [[dict(seq=NSEQ, L=TS, slot=0, ci=0, first=True, last=True, pidx=NCT)]] + units

    wq = {"n": 0}

    def load_weights(l, m):
        i = wq["n"] % 2
        wq["n"] += 1
        c0, ncol = WCOLS[m]
        src = w_in[l].rearrange("(k p) c -> p k c", p=128)
        half = (ncol + 1) // 2
        dma("pool", wslot[i][:, :, 0:half], src[:, :, c0:c0 + half], W=(wslot[i],))
        dma("pool", wslot[i][:, :, half:ncol], src[:, :, c0 + half:c0 + ncol], W=(wslot[i],))
        mi = MIX.index(m)
        srco = w_out[l, mi * 512:(mi + 1) * 512, :].rearrange("(k p) c -> p k c", p=128)
        dma("pool", woslot[i][:], srco, W=(woslot[i],))
        a = aux[i]
        if m == "M":
            dma("sp", a[:, 0:512], w_mnw[l:l + 1, :].partition_broadcast(128), W=(a,))
            dma("sp", a[:, 512:520], w_gb[l:l + 1, :].partition_broadcast(128), W=(a,))
        elif m == "S":
            dma("sp", a[:, 0:512], w_snw[l:l + 1, :].partition_broadcast(128), W=(a,))
            sa = saux[i]
            dma("sp", sa[:, 0:8], w_dtb[l:l + 1, :].partition_broadcast(128), W=(sa,))
            dma("sp", sa[:, 8:16], w_alog[l:l + 1, :].partition_broadcast(128), W=(sa,))
            dma("sp", sa[:, 16:24], w_sd[l:l + 1, :].partition_broadcast(128), W=(sa,))
            dma("sp", a[:, 512:1536], w_cb[l:l + 1, :].partition_broadcast(128), W=(a,))
        elif m == "H":
            dma("sp", a[:, 0:512], w_hnw[l:l + 1, :].partition_broadcast(128), W=(a,))
            if l == 1:
                dma("sp", a[:, 512:1024], w_lb[0:1, :].partition_broadcast(128), W=(a,))
                dma("sp", a[:, 1024:1536], w_lb[1:2, :].partition_broadcast(128), W=(a,))
        return i

    def alt():
        return tick["n"] % 2

    def next_chunk():
        tick["n"] += 1

    def norm_chunk(c, wbc, out_bf, ev):
        L = c["L"]; xt = x_t[c["slot"]]
        s = smsets[alt()][0]
        act(junk[:L, :], xt[:L, :], AF.Square, (xt,), (junk, s), accum=s[:L, 0:1])
        act(s[:L, 1:2], s[:L, 0:1], AF.Ln, (s,), (s,), bias=EPS, scale=1.0 / D)
        act(s[:L, 2:3], s[:L, 1:2], AF.Exp, (s,), (s,), scale=-0.5)
        stt("dve", out_bf[:L, :], xt[:L, :], s[:L, 2:3], wbc[:L, :], ALU.mult, ALU.mult, (xt, s, wbc), (out_bf,))

    def layer_norm_transpose(c):
        L = c["L"]; slot = c["slot"]
        hb = hn_b[alt()]
        norm_chunk(c, normw, hb, None)
        bk = banks[2]
        v = bfv(bk)
        for k in range(8):
            tr(v[:, k * 128:k * 128 + L], hb[:L, k * 128:(k + 1) * 128], ident[:L, :L], (hb, ident), (bk,))
        E("act", lambda: nc.scalar.copy(out=hnT[:, :, slot * 128:slot * 128 + L],
                                        in_=v[:, 0:1024].rearrange("p (k t) -> p k t", t=128)[:, :, 0:L]),
          (bk,), (hnT_r[slot],))

    def project(c, wi, g0, gn, bank):
        L = c["L"]; slot = c["slot"]
        for k in range(8):
            mm(bank[:L, 0:gn], hnT[:, k, slot * 128:slot * 128 + L], wslot[wi][:, k, g0:g0 + gn],
               k == 0, k == 7, (hnT_r[slot], wslot[wi]), (bank,))

    def out_proj(c, wi, omt, accb=(3, 4), trb=6):
        L = c["L"]; xt = x_t[c["slot"]]
        bk = banks[trb]; v = bfv(bk)
        for k in range(4):
            tr(v[:, k * 128:k * 128 + L], omt[:L, k * 128:(k + 1) * 128], ident[:L, :L], (omt, ident), (bk,))
        oT = omT[alt()]
        E("act", lambda: nc.scalar.copy(out=oT[:, :, 0:L], in_=v[:, 0:512].rearrange("p (k t) -> p k t", t=128)[:, :, 0:L]),
          (bk,), (oT,))
        for g in range(2):
            bank = banks[accb[g]]
            for k in range(4):
                mm(bank[:L, :], oT[:, k, 0:L], woslot[wi][:, k, g * 512:(g + 1) * 512], k == 0, k == 3,
                   (oT, woslot[wi]), (bank,))
            tt("dve", xt[:L, g * 512:(g + 1) * 512], xt[:L, g * 512:(g + 1) * 512], bank[:L, :], ALU.add,
               (xt, bank), (xt,))

    def hilo(src_ap, L, n, dst, R):
        vcopy("dve", dst[:L, 0, 0:n], src_ap, R, (dst,))
        tt("dve", dst[:L, 1, 0:n], src_ap, dst[:L, 0, 0:n], ALU.subtract, tuple(R) + (dst,), (dst,))

    def mlstm_chunk(c, l, wi):
        L = c["L"]; a = aux[wi]; pidx = c["pidx"]
        qb_ = qk_b[alt()]; vb = v_b[alt()]
        s0, s1, s2, s3 = smsets[alt()]
        project(c, wi, 0, 512, banks[0])
        E("act", lambda b_=banks[0]: nc.scalar.activation(out=qb_[:L, 0:256], in_=b_[:L, 0:256], func=AF.Copy, scale=0.125), (banks[0],), (qb_,))
        E("act", lambda b_=banks[0]: nc.scalar.copy(out=qb_[:L, 256:512], in_=b_[:L, 256:512]), (banks[0],), (qb_,))
        project(c, wi, 2048, 8, banks[7])
        tt("dve", s0[:L, 0:8], banks[7][:L, 0:8], a[:L, 512:520], ALU.add, (banks[7], a), (s0,))
        act(s0[:L, 8:12], s0[:L, 4:8], AF.Exp, (s0,), (s0,), scale=-1.0)
        act(s0[:L, 12:16], s0[:L, 8:12], AF.Ln, (s0,), (s0,), bias=1.0)
        ts("dve", s0[:L, 16:20], s0[:L, 12:16], -1.0, ALU.mult, (s0,), (s0,))
        lh = smb[alt()]
        vcopy("dve", lh[:L, 0:4], s0[:L, 16:20], (s0,), (lh,))
        tt("dve", lh[:L, 4:8], s0[:L, 16:20], lh[:L, 0:4], ALU.subtract, (s0, lh), (lh,))
        bk = banks[7]
        mm(bk[:L, 16:20], tri[:L, :L], lh[:L, 0:4], True, False, (tri, lh), (bk,))
        mm(bk[:L, 16:20], tri[:L, :L], lh[:L, 4:8], False, True, (tri, lh), (bk,))
        mm(bk[:, 32:36], ones[:L, :], lh[:L, 0:4], True, False, (ones, lh), (bk,))
        mm(bk[:, 32:36], ones[:L, :], lh[:L, 4:8], False, True, (ones, lh), (bk,))
        vcopy("dve", s1[:, 0:4], bk[:, 32:36], (bk,), (s1,))
        vcopy("dve", s1[:L, 4:8], bk[:L, 16:20], (bk,), (s1,))
        act(s1[:L, 8:12], s1[:L, 4:8], AF.Exp, (s1,), (s1,))
        tt("dve", s1[:L, 12:16], s0[:L, 0:4], s1[:L, 4:8], ALU.subtract, (s0, s1), (s1,))
        act(s1[:L, 16:20], s1[:L, 12:16], AF.Exp, (s1,), (s1,))
        tt("dve", s1[:L, 20:24], s1[:L, 12:16], s1[:L, 0:4], ALU.add, (s1,), (s1,))
        act(s1[:64, 24:28], s1[:64, 0:4], AF.Exp, (s1,), (s1,))
        bk6 = banks[6]; v6 = bfv(bk6)
        vcopy("dve", lh[:L, 16:20], s1[:L, 20:24], (s1,), (lh,))
        tt("dve", lh[:L, 20:24], s1[:L, 20:24], lh[:L, 16:20], ALU.subtract, (s1, lh), (lh,))
        vcopy("dve", lh[:L, 24:28], s1[:L, 4:8], (s1,), (lh,))
        tt("dve", lh[:L, 28:32], s1[:L, 4:8], lh[:L, 24:28], ALU.subtract, (s1, lh), (lh,))
        for j in range(4):
            tr(v6[0:4, j * 128:j * 128 + L], lh[:L, 16 + 4 * j:20 + 4 * j], ident[:L, :L], (lh, ident), (bk6,))
        vcopy("dve", tstat[0:4, 0:L], v6[0:4, 0:L], (bk6,), (tstat,))
        tt("dve", tstat[0:4, 0:L], tstat[0:4, 0:L], v6[0:4, 128:128 + L], ALU.add, (tstat, bk6), (tstat,))
        E("dve", lambda: nc.vector.tensor_reduce(out=mG[l][0:4, pidx:pidx + 1], in_=tstat[0:4, 0:L], axis=AX.X, op=ALU.max),
          (tstat,), (mG[l],))
        vcopy("dve", tstat[0:4, 130:131], v6[0:4, 256 + L - 1:256 + L], (bk6,), (tstat,))
        tt("dve", mB[l][0:4, pidx:pidx + 1], tstat[0:4, 130:131], v6[0:4, 384 + L - 1:384 + L], ALU.add, (tstat, bk6), (mB[l],))
        project(c, wi, 512, 512, banks[1])
        tt("dve", vb[:L, :, 0:128], banks[1][:L, :].rearrange("p (h v) -> p h v", v=128),
           s1[:L, 16:20].unsqueeze(2).to_broadcast([L, 4, 128]), ALU.mult, (banks[1], s1), (vb,))
        vcopy("dve", vb[:L, :, 128:129], s1[:L, 16:20].unsqueeze(2), (s1,), (vb,))
        bkt = banks[2]; v2 = bfv(bkt)
        for j in range(8):
            tr(v2[0:64, j * 128:j * 128 + L], qb_[:L, j * 64:(j + 1) * 64], ident[:L, :L], (qb_, ident), (bkt,))
        qT = qkT[alt()]
        E("act", lambda: nc.scalar.copy(out=qT[:, :, 0:L], in_=v2[0:64, 0:1024].rearrange("p (j t) -> p j t", t=128)[:, :, 0:L]),
          (bkt,), (qT,))
        bs = banks[3]
        for h in range(4):
            mm(bs[:L, h * 128:h * 128 + L], qT[:, 4 + h, 0:L], qT[:, h, 0:L], True, True, (qT,), (bs,))
        pt = PT[alt()]
        tt("dve", pt[:L, :, 0:L], bs[:L, :].rearrange("p (h t) -> p h t", t=128)[:, :, 0:L],
           tri_f[:L, 0:L].unsqueeze(1).to_broadcast([L, 4, L]), ALU.mult, (bs, tri_f), (pt,))
        for h in range(4):
            bo = banks[4 + h // 2]; o0 = (h % 2) * 129
            mm(bo[:L, o0:o0 + 129], pt[:L, h, 0:L], vb[:L, h, :], True, False, (pt, vb), (bo,))
            mm(bo[:L, o0:o0 + 129], qT[:, h, 0:L], Cbf[:, h, :], False, True, (qT, Cbf), (bo,))
        for h in range(4):
            bu = banks[6 + h // 2]; o0 = (h % 2) * 129
            mm(bu[0:64, o0:o0 + 129], qb_[:L, 256 + h * 64:256 + (h + 1) * 64], vb[:L, h, :], True, True, (qb_, vb), (bu,))
        C = Cst[l]
        for hp in range(2):
            bu = banks[6 + hp]
            tt("dve", C[:, 2 * hp:2 * hp + 2, :], C[:, 2 * hp:2 * hp + 2, :],
               bu[0:64, 0:258].rearrange("p (h v) -> p h v", v=129), ALU.add, (C, bu), (C,))
        tt("dve", C[:, :, :], C[:, :, :], s1[:64, 24:28].unsqueeze(2).to_broadcast([64, 4, 129]), ALU.mult, (C, s1), (C,))
        E("act", lambda: nc.scalar.copy(out=Cbf[:], in_=C[:]), (C,), (Cbf,))
        for hp in range(2):
            bo = banks[4 + hp]
            vcopy("dve", s2[:L, 2 * hp:2 * hp + 2], bo[:L, 0:258].rearrange("p (h v) -> p h v", v=129)[:, :, 128], (bo,), (s2,))
        tt("dve", s2[:L, 4:8], s2[:L, 0:4], s1[:L, 8:12], ALU.mult, (s2, s1), (s2,))
        tt("dve", s2[:L, 8:12], s2[:L, 4:8], s2[:L, 4:8], ALU.mult, (s2,), (s2,))
        ts("dve", s2[:L, 8:12], s2[:L, 8:12], 1.0, ALU.max, (s2,), (s2,))
        act(s2[:L, 40:44], s2[:L, 8:12], AF.Ln, (s2,), (s2,))
        act(s2[:L, 12:16], s2[:L, 40:44], AF.Exp, (s2,), (s2,), scale=-0.5)
        tt("dve", s2[:L, 16:20], s2[:L, 12:16], s1[:L, 8:12], ALU.mult, (s2, s1), (s2,))
        for h in range(4):
            bo = banks[4 + h // 2]; o0 = (h % 2) * 129
            act(junk[:L, 0:128], bo[:L, o0:o0 + 128], AF.Square, (bo,), (junk, s3), accum=s3[:L, h:h + 1])
        tt("dve", s2[:L, 20:24], s2[:L, 16:20], s2[:L, 16:20], ALU.mult, (s2,), (s2,))
        tt("dve", s2[:L, 24:28], s2[:L, 20:24], s3[:L, 0:4], ALU.mult, (s2, s3), (s2,))
        act(s2[:L, 28:32], s2[:L, 24:28], AF.Ln, (s2,), (s2,), bias=EPS, scale=1.0 / 128)
        act(s2[:L, 32:36], s2[:L, 28:32], AF.Exp, (s2,), (s2,), scale=-0.5)
        tt("dve", s2[:L, 36:40], s2[:L, 32:36], s2[:L, 16:20], ALU.mult, (s2,), (s2,))
        gg = g1[alt()]
        for h in range(4):
            bo = banks[4 + h // 2]; o0 = (h % 2) * 129
            stt("dve", gg[:L, h * 128:(h + 1) * 128], bo[:L, o0:o0 + 128], s2[:L, 36 + h:37 + h],
                a[:L, h * 128:(h + 1) * 128], ALU.mult, ALU.mult, (bo, s2, a), (gg,))
        project(c, wi, 1024, 512, banks[7])
        act(e1[:L, 0:512], banks[7][:L, :], AF.Exp, (banks[7],), (e1,), scale=-1.0)
        project(c, wi, 1536, 512, banks[6])
        act(e1[:L, 512:1024], banks[6][:L, :], AF.Exp, (banks[6],), (e1,), scale=-1.0)
        act(e1[:L, :], e1[:L, :], AF.Ln, (e1,), (e1,), bias=1.0)
        tt("pool", e2[:L, :], e1[:L, 0:512], e1[:L, 512:1024], ALU.add, (e1,), (e2,))
        act(e2[:L, :], e2[:L, :], AF.Exp, (e2,), (e2,), scale=-1.0)
        tt("dve", e2[:L, :], e2[:L, :], banks[6][:L, :], ALU.mult, (e2, banks[6]), (e2,))
        omt = om[alt()]
        tt("dve", omt[:L, :], e2[:L, :], gg[:L, :], ALU.mult, (e2, gg), (omt,))
        out_proj(c, wi, omt, (3, 4), 5)

    def mlstm_init(c, l):
        C = Cst[l]
        if c["seq"] < NSEQ:
            memset("pool", C, C[:], 0.0)
            memset("pool", Cbf, Cbf[:], 0.0)
            memset("pool", m0[l], m0[l][:], -1e30)
        else:
            dma("sp", C[:, :, 0:128], st_mc[l].rearrange("h k v -> k h v"), W=(C,))
            dma("sp", C[:, :, 128:129], st_mn[l].rearrange("h (k o) -> k h o", o=1), W=(C,), nonc=True)
            dma("sp", m0[l][:], st_mm[l].rearrange("(h o) -> h o", o=1), W=(m0[l],), nonc=True)
            s = sm[3]
            dma("sp", s[0:64, 32:36], st_mm[l:l + 1, :].partition_broadcast(64), W=(s,))
            act(s[0:64, 36:40], s[0:64, 32:36], AF.Exp, (s,), (s,))
            tt("dve", C[:, :, :], C[:, :, :], s[0:64, 36:40].unsqueeze(2).to_broadcast([64, 4, 129]), ALU.mult, (C, s), (C,))
            E("act", lambda: nc.scalar.copy(out=Cbf[:], in_=C[:]), (C,), (Cbf,))

    def mlstm_final(c, l, nchunks, p0):
        sq = c["seq"]; ts_ = tstat; n = nchunks
        G = mG[l]; B = mB[l]
        E("dve", lambda: nc.vector.tensor_tensor_scan(out=ts_[0:4, 0:n], data0=ones_f4[0:4, 0:n], data1=B[0:4, p0:p0 + n],
                                                     initial=0.0, op0=ALU.mult, op1=ALU.add), (B, ones_f4), (ts_,))
        tt("dve", ts_[0:4, 32:32 + n], G[0:4, p0:p0 + n], ts_[0:4, 0:n], ALU.subtract, (G, ts_), (ts_,))
        ts("dve", ts_[0:4, 32:32 + n], ts_[0:4, 32:32 + n], ts_[0:4, n - 1:n], ALU.add, (ts_,), (ts_,))
        E("dve", lambda: nc.vector.tensor_reduce(out=ts_[0:4, 64:65], in_=ts_[0:4, 32:32 + n], axis=AX.X, op=ALU.max), (ts_,), (ts_,))
        tt("dve", ts_[0:4, 65:66], m0[l][0:4, 0:1], ts_[0:4, n - 1:n], ALU.add, (m0[l], ts_), (ts_,))
        tt("dve", ts_[0:4, 66:67], ts_[0:4, 64:65], ts_[0:4, 65:66], ALU.max, (ts_,), (ts_,))
        dma("sp", o_mm[l, sq].rearrange("(h o) -> h o", o=1), ts_[0:4, 66:67], R=(ts_,), nonc=True)
        d1 = dma("sp", scr_m[l, sq].rearrange("(h o) -> h o", o=1), ts_[0:4, 66:67], R=(ts_,), W=(scr_reg,), nonc=True)
        s = sm[3]
        dma("sp", s[0:64, 40:44], scr_m[l, sq:sq + 1, :].partition_broadcast(64), R=(scr_reg,), W=(s,))
        act(s[0:64, 44:48], s[0:64, 40:44], AF.Exp, (s,), (s,), scale=-1.0)
        C = Cst[l]
        co = cout
        tt("dve", co[:, :, :], C[:, :, :], s[0:64, 44:48].unsqueeze(2).to_broadcast([64, 4, 129]), ALU.mult, (C, s), (co,))
        dma("sp", o_mc[l, sq].rearrange("h k v -> k h v"), co[:, :, 0:128], R=(co,))
        dma("sp", o_mn[l, sq].rearrange("h (k o) -> k h o", o=1), co[:, :, 128:129], R=(co,), nonc=True)

    ones_f4 = sb("ones_f4", [4, 32]); memset("dve", ones_f4, ones_f4[:], 1.0)
    scr_reg = T(None)
    cout = T(e1[0:64, 0:516].rearrange("p (h v) -> p h v", v=129)); cout.reg = e1.reg; cout.arena = True

    def ret_chunk(c, l, wi):
        L = c["L"]; pidx = c["pidx"]
        project(c, wi, 0, 512, banks[0])
        b0 = banks[0]
        x4 = b0[:L, :].rearrange("p (j r d) -> p j r d", r=2, d=32)
        cb_ = cosT[:L, pidx, :].unsqueeze(1).to_broadcast([L, 8, 32])
        sb_ = sinT[:L, pidx, :].unsqueeze(1).to_broadcast([L, 8, 32])
        r4 = rot[:L, :, :].rearrange("p j (r d) -> p j r d", r=2)
        tt("dve", rt1[:L], x4[:, :, 0, :], cb_, ALU.mult, (b0, cosT), (rt1,))
        tt("dve", rt2[:L], x4[:, :, 1, :], sb_, ALU.mult, (b0, sinT), (rt2,))
        tt("dve", r4[:, :, 0, :], rt1[:L], rt2[:L], ALU.subtract, (rt1, rt2), (rot,))
        tt("dve", rt1[:L], x4[:, :, 0, :], sb_, ALU.mult, (b0, sinT), (rt1,))
        tt("dve", rt2[:L], x4[:, :, 1, :], cb_, ALU.mult, (b0, cosT), (rt2,))
        tt("dve", r4[:, :, 1, :], rt1[:L], rt2[:L], ALU.add, (rt1, rt2), (rot,))
        qb_ = qk_b[alt()]
        tt("dve", qb_[:L, :].rearrange("p (j d) -> p j d", d=64), rot[:L], gqk[:L, 0:8].unsqueeze(2).to_broadcast([L, 8, 64]),
           ALU.mult, (rot, gqk), (qb_,))
        bkt = banks[2]; v2 = bfv(bkt)
        for j in range(8):
            tr(v2[0:64, j * 128:j * 128 + L], qb_[:L, j * 64:(j + 1) * 64], ident[:L, :L], (qb_, ident), (bkt,))
        qT = qkT[alt()]
        E("act", lambda: nc.scalar.copy(out=qT[:, :, 0:L], in_=v2[0:64, 0:1024].rearrange("p (j t) -> p j t", t=128)[:, :, 0:L]),
          (bkt,), (qT,))
        bs = banks[3]
        for h in range(4):
            mm(bs[:L, h * 128:h * 128 + L], qT[:, 4 + h, 0:L], qT[:, h, 0:L], True, True, (qT,), (bs,))
        pt = PT[alt()]
        tt("dve", pt[:L, :, 0:L], bs[:L, :].rearrange("p (h t) -> p h t", t=128)[:, :, 0:L],
           tri_f[:L, 0:L].unsqueeze(1).to_broadcast([L, 4, L]), ALU.mult, (bs, tri_f), (pt,))
        project(c, wi, 512, 512, banks[1])
        vb = v_b[alt()]
        E("act", lambda b_=banks[1]: nc.scalar.copy(out=vb[:L, :, 0:128], in_=b_[:L, :].rearrange("p (h v) -> p h v", v=128)),
          (banks[1],), (vb,))
        bo = banks[4]
        for h in range(4):
            mm(bo[:L, h * 128:(h + 1) * 128], pt[:L, h, 0:L], vb[:L, h, 0:128], True, False, (pt, vb), (bo,))
            mm(bo[:L, h * 128:(h + 1) * 128], qT[:, h, 0:L], Rbf[:, h, :], False, True, (qT, Rbf), (bo,))
        bu = banks[6]
        for h in range(4):
            mm(bu[0:64, h * 128:(h + 1) * 128], qb_[:L, 256 + h * 64:256 + (h + 1) * 64], vb[:L, h, 0:128], True, True,
               (qb_, vb), (bu,))
        Rs = Rst[l]
        gcol = 0 if L == 128 else 4
        tt("dve", Rs[:], Rs[:], bu[0:64, :].rearrange("p (h v) -> p h v", v=128), ALU.add, (Rs, bu), (Rs,))
        tt("dve", Rs[:], Rs[:], gl[:, gcol:gcol + 4].unsqueeze(2).to_broadcast([64, 4, 128]), ALU.mult, (Rs, gl), (Rs,))
        E("act", lambda: nc.scalar.copy(out=Rbf[:], in_=Rs[:]), (Rs,), (Rbf,))
        s2 = smsets[alt()][2]; s3 = smsets[alt()][3]
        o3 = bo[:L, :].rearrange("p (h v) -> p h v", v=128)
        E("dve", lambda: nc.vector.tensor_reduce(out=s2[:L, 0:4], in_=o3, axis=AX.X, op=ALU.add), (bo,), (s2,))
        ts("dve", s2[:L, 4:8], s2[:L, 0:4], -1.0 / 128, ALU.mult, (s2,), (s2,))
        gg = g1[alt()]
        for h in range(4):
            act(gg[:L, h * 128:(h + 1) * 128], bo[:L, h * 128:(h + 1) * 128], AF.Identity, (bo, s2), (gg,), bias=s2[:L, 4 + h:5 + h])
            act(junk[:L, 0:128], gg[:L, h * 128:(h + 1) * 128], AF.Square, (gg,), (junk, s3), accum=s3[:L, 8 + h:9 + h])
        act(s2[:L, 8:12], s3[:L, 8:12], AF.Ln, (s3,), (s2,), bias=EPS, scale=1.0 / 128)
        act(s2[:L, 12:16], s2[:L, 8:12], AF.Exp, (s2,), (s2,), scale=-0.5)
        project(c, wi, 1024, 512, banks[5])
        act(e1[:L, 0:512], banks[5][:L, :], AF.Exp, (banks[5],), (e1,), scale=-1.0)
        act(e1[:L, 0:512], e1[:L, 0:512], AF.Ln, (e1,), (e1,), bias=1.0)
        act(e2[:L, :], e1[:L, 0:512], AF.Exp, (e1,), (e2,), scale=-1.0)
        tt("dve", e2[:L, :], e2[:L, :], banks[5][:L, :], ALU.mult, (e2, banks[5]), (e2,))
        tt("dve", gg[:L, :].rearrange("p (h v) -> p h v", v=128), gg[:L, :].rearrange("p (h v) -> p h v", v=128),
           s2[:L, 12:16].unsqueeze(2).to_broadcast([L, 4, 128]), ALU.mult, (gg, s2), (gg,))
        omt = om[alt()]
        tt("dve", omt[:L, :], gg[:L, :], e2[:L, :], ALU.mult, (gg, e2), (omt,))
        out_proj(c, wi, omt, (7, 5))

    def ret_init(c, l):
        Rs = Rst[l]
        if c["seq"] < NSEQ:
            memset("pool", Rs, Rs[:], 0.0)
            memset("pool", Rbf, Rbf[:], 0.0)
        else:
            dma("sp", Rs[:], st_r[l].rearrange("h k v -> k h v"), W=(Rs,))
            E("act", lambda: nc.scalar.copy(out=Rbf[:], in_=Rs[:]), (Rs,), (Rbf,))

    def ret_final(c, l):
        dma("sp", o_r[l, c["seq"]].rearrange("h k v -> k h v"), Rst[l][:], R=(Rst[l],))

    def ssd_phase_setup(l, wi):
        a = aux[wi]
        for t8 in range(8):
            for w in range(4):
                idx = t8 * 4 + w
                ts("dve", cdiag[:, idx, :], ident_f[:, :], cwall[:, l, t8, w:w + 1], ALU.mult, (ident_f, cwall), (cdiag,))
        sa = saux[wi]
        act(sa[:, 24:32], sa[:, 8:16], AF.Exp, (sa,), (sa,))
        ts("dve", sa[:, 24:32], sa[:, 24:32], -1.0, ALU.mult, (sa,), (sa,))

    def ssd_chunk(c, l, wi):
        L = c["L"]; a = aux[wi]; xc = xcT[l]
        s0, s1, s2, s3 = smsets[alt()]
        project(c, wi, 512, 512, banks[0])
        E("act", lambda b_=banks[0]: nc.scalar.copy(out=xbc_b[:L, 0:512], in_=b_[:L, :]), (banks[0],), (xbc_b,))
        project(c, wi, 1024, 512, banks[1])
        E("act", lambda b_=banks[1]: nc.scalar.copy(out=xbc_b[:L, 512:1024], in_=b_[:L, :]), (banks[1],), (xbc_b,))
        if c["last"]:
            dma("pool", o_cv[l, c["seq"]], xbc_b[L - 3:L, :], R=(xbc_b,))
        bkt = banks[2]; v2 = bfv(bkt)
        for k in range(8):
            tr(v2[:, k * 128:k * 128 + L], xbc_b[:L, k * 128:(k + 1) * 128], ident[:L, :L], (xbc_b, ident), (bkt,))
        E("act", lambda: nc.scalar.copy(out=xc[:, :, 3:3 + L], in_=v2[:, 0:1024].rearrange("p (k t) -> p k t", t=128)[:, :, 0:L]),
          (bkt,), (xc,))
        xbt = xb_tok[alt()]
        for t8 in range(8):
            bo = banks[4 + t8 // 4]; o0 = (t8 % 4) * 128
            for w in range(4):
                mm(bo[:L, o0:o0 + 128], xc[:, t8, w:w + L], cdiag[:, t8 * 4 + w, :], w == 0, w == 3, (xc, cdiag), (bo,))
        vcopy("dve", xc[:, :, 0:3], xc[:, :, L:L + 3], (xc,), (xc,))
        dflat = decT[:L, :, :].rearrange("p h t -> p (h t)")
        for hp in range(2):
            tt("dve", e1[:L, hp * 512:(hp + 1) * 512], banks[4 + hp][:L, :], a[:L, 512 + hp * 512:1024 + hp * 512], ALU.add,
               (banks[4 + hp], a), (e1,))
        act(dflat, e1[:L, :], AF.Exp, (e1,), (decT,), scale=-1.0)
        act(dflat, dflat, AF.Ln, (decT,), (decT,), bias=1.0)
        act(dflat, dflat, AF.Exp, (decT,), (decT,), scale=-1.0)
        tt("dve", xbt[:L, :], e1[:L, :], dflat, ALU.mult, (e1, decT), (xbt,))
        bkt2 = banks[2]; v3 = bfv(bkt2)
        for k in range(4):
            tr(v3[:, k * 128:k * 128 + L], xbt[:L, 512 + k * 128:512 + (k + 1) * 128], ident[:L, :L], (xbt, ident), (bkt2,))
        xa = xact[alt()]
        E("act", lambda: nc.scalar.copy(out=xa[:, 4:8, 0:L], in_=v3[:, 0:512].rearrange("p (k t) -> p k t", t=128)[:, :, 0:L]),
          (bkt2,), (xa,))
        project(c, wi, 1536, 8, banks[7])
        bk = banks[7]
        sa = saux[wi]
        tt("dve", s0[:L, 0:8], bk[:L, 0:8], sa[:L, 0:8], ALU.add, (bk, sa), (s0,))
        act(s0[:L, 8:16], s0[:L, 0:8], AF.Exp, (s0,), (s0,))
        act(s0[:L, 16:24], s0[:L, 8:16], AF.Ln, (s0,), (s0,), bias=1.0)
        tt("dve", s0[:L, 24:32], s0[:L, 16:24], sa[:L, 24:32], ALU.mult, (s0, sa), (s0,))
        lh = smb[alt()]
        vcopy("dve", lh[:L, 0:8], s0[:L, 24:32], (s0,), (lh,))
        tt("dve", lh[:L, 8:16], s0[:L, 24:32], lh[:L, 0:8], ALU.subtract, (s0, lh), (lh,))
        mm(bk[:L, 16:24], tri[:L, :L], lh[:L, 0:8], True, False, (tri, lh), (bk,))
        mm(bk[:L, 16:24], tri[:L, :L], lh[:L, 8:16], False, True, (tri, lh), (bk,))
        mm(bk[:, 32:40], ones[:L, :], lh[:L, 0:8], True, False, (ones, lh), (bk,))
        mm(bk[:, 32:40], ones[:L, :], lh[:L, 8:16], False, True, (ones, lh), (bk,))
        vcopy("dve", s1[:, 0:8], bk[:, 32:40], (bk,), (s1,))
        vcopy("dve", s1[:L, 8:16], bk[:L, 16:24], (bk,), (s1,))
        act(s1[:L, 16:24], s1[:L, 8:16], AF.Exp, (s1,), (s1,))
        tt("dve", s1[:L, 24:32], s1[:L, 0:8], s1[:L, 8:16], ALU.subtract, (s1,), (s1,))
        act(s1[:L, 32:40], s1[:L, 24:32], AF.Exp, (s1,), (s1,))
        act(s1[:, 40:48], s1[:, 0:8], AF.Exp, (s1,), (s1,))
        for part in range(2):
            tt("dve", larhs[:L, part, :, 0:L], tri_f[:L, 0:L].unsqueeze(1).to_broadcast([L, 8, L]),
               lh[:L, part * 8:(part + 1) * 8].unsqueeze(2).to_broadcast([L, 8, L]), ALU.mult, (tri_f, lh), (larhs,))
        for h in range(8):
            bo = banks[4 + h // 4]; o0 = (h % 4) * 128
            mm(bo[:L, o0:o0 + L], ust[:L, :L], larhs[:L, 0, h, 0:L], True, False, (ust, larhs), (bo,))
            mm(bo[:L, o0:o0 + L], ust[:L, :L], larhs[:L, 1, h, 0:L], False, True, (ust, larhs), (bo,))
        for hp in range(2):
            bo = banks[4 + hp]
            act(decT[:L, 4 * hp:4 * hp + 4, 0:L], bo[:L, :].rearrange("p (h t) -> p h t", t=128)[:, :, 0:L], AF.Exp, (bo,), (decT,))
        bs = banks[6]
        for g in range(2):
            mm(bs[:L, g * 128:g * 128 + L], xa[:, 4 + g, 0:L], xa[:, 6 + g, 0:L], True, True, (xa,), (bs,))
        tt("dve", cbm[:L, :, 0:L], bs[:L, 0:256].rearrange("p (g t) -> p g t", t=128)[:, :, 0:L],
           tri_f[:L, 0:L].unsqueeze(1).to_broadcast([L, 2, L]), ALU.mult, (bs, tri_f), (cbm,))
        sc = scT[alt()]
        for g in range(2):
            tt("dve", sc[:L, 4 * g:4 * g + 4, 0:L], decT[:L, 4 * g:4 * g + 4, 0:L],
               cbm[:L, g, 0:L].unsqueeze(1).to_broadcast([L, 4, L]), ALU.mult, (decT, cbm), (sc,))
        xd = xdt[alt()]; xw_ = xw[alt()]
        tt("dve", xd[:L, :].rearrange("p (h d) -> p h d", d=64), xbt[:L, 0:512].rearrange("p (h d) -> p h d", d=64),
           s0[:L, 16:24].unsqueeze(2).to_broadcast([L, 8, 64]), ALU.mult, (xbt, s0), (xd,))
        tt("dve", xw_[:L, :].rearrange("p (h d) -> p h d", d=64), xd[:L, :].rearrange("p (h d) -> p h d", d=64),
           s1[:L, 32:40].unsqueeze(2).to_broadcast([L, 8, 64]), ALU.mult, (xd, s1), (xw_,))
        b0 = banks[4]; b1 = banks[5]
        for h in range(8):
            mm(b0[:L, h * 64:(h + 1) * 64], sc[:L, h, 0:L], xd[:L, h * 64:(h + 1) * 64], True, True, (sc, xd), (b0,))
        for g in range(2):
            mm(b1[:L, g * 256:(g + 1) * 256], xa[:, 6 + g, 0:L], Hbf[:, 4 * g:4 * g + 4, :].rearrange("p h d -> p (h d)"), True, True, (xa, Hbf), (b1,))
        tt("dve", yin[:L, :].rearrange("p (h d) -> p h d", d=64), b1[:L, :].rearrange("p (h d) -> p h d", d=64),
           s1[:L, 16:24].unsqueeze(2).to_broadcast([L, 8, 64]), ALU.mult, (b1, s1), (yin,))
        tt("dve", yin[:L, :], yin[:L, :], b0[:L, :], ALU.add, (yin, b0), (yin,))
        tt("dve", e2[:L, :].rearrange("p (h d) -> p h d", d=64), xbt[:L, 0:512].rearrange("p (h d) -> p h d", d=64),
           sa[:L, 16:24].unsqueeze(2).to_broadcast([L, 8, 64]), ALU.mult, (xbt, sa), (e2,))
        tt("dve", yin[:L, :], yin[:L, :], e2[:L, :], ALU.add, (yin, e2), (yin,))
        bu = banks[3]
        for g in range(2):
            mm(bu[:, g * 256:(g + 1) * 256], xbt[:L, 512 + g * 128:512 + (g + 1) * 128], xw_[:L, g * 256:(g + 1) * 256],
               True, True, (xbt, xw_), (bu,))
        Hs = Hst[l]
        tt("dve", Hs[:], Hs[:], s1[:, 40:48].unsqueeze(2).to_broadcast([128, 8, 64]), ALU.mult, (Hs, s1), (Hs,))
        tt("dve", Hs[:], Hs[:], bu[:, :].rearrange("p (h d) -> p h d", d=64), ALU.add, (Hs, bu), (Hs,))
        E("act", lambda: nc.scalar.copy(out=Hbf[:], in_=Hs[:]), (Hs,), (Hbf,))
        project(c, wi, 0, 512, banks[6])
        act(e1[:L, 0:512], banks[6][:L, :], AF.Exp, (banks[6],), (e1,), scale=-1.0)
        act(e1[:L, 0:512], e1[:L, 0:512], AF.Ln, (e1,), (e1,), bias=1.0)
        act(e1[:L, 512:1024], e1[:L, 0:512], AF.Exp, (e1,), (e1,), scale=-1.0)
        tt("dve", e1[:L, 512:1024], e1[:L, 512:1024], banks[6][:L, :], ALU.mult, (e1, banks[6]), (e1,))
        tt("dve", yin[:L, :], yin[:L, :], e1[:L, 512:1024], ALU.mult, (yin, e1), (yin,))
        for g in range(2):
            act(junk[:L, 0:256], yin[:L, g * 256:(g + 1) * 256], AF.Square, (yin,), (junk, s3), accum=s3[:L, 16 + g:17 + g])
        act(s2[:L, 0:2], s3[:L, 16:18], AF.Ln, (s3,), (s2,), bias=EPS, scale=1.0 / 256)
        act(s2[:L, 2:4], s2[:L, 0:2], AF.Exp, (s2,), (s2,), scale=-0.5)
        omt = om[alt()]
        for g in range(2):
            stt("dve", omt[:L, g * 256:(g + 1) * 256], yin[:L, g * 256:(g + 1) * 256], s2[:L, 2 + g:3 + g],
                a[:L, g * 256:(g + 1) * 256], ALU.mult, ALU.mult, (yin, s2, a), (omt,))
        out_proj(c, wi, omt, (3, 6))

    def ssd_init(c, l):
        Hs = Hst[l]; xc = xcT[l]
        if c["seq"] < NSEQ:
            memset("pool", Hs, Hs[:], 0.0)
            memset("pool", Hbf, Hbf[:], 0.0)
            memset("dve", xc, xc[:, :, 0:3], 0.0)
        else:
            dma("sp", Hs[:], st_s[l].rearrange("h n p -> n h p"), W=(Hs,))
            E("act", lambda: nc.scalar.copy(out=Hbf[:], in_=Hs[:]), (Hs,), (Hbf,))
            for w_ in range(3):
                dma("sp", e2[:, w_ * 8:(w_ + 1) * 8].unsqueeze(2), st_cv[l, w_].rearrange("(a p o) -> p a o", p=128, o=1), W=(e2,), nonc=True)
            vcopy("dve", xc[:, :, 0:3], e2[:, 0:24].rearrange("p (w a) -> p a w", w=3), (e2,), (xc,))

    def ssd_final(c, l):
        dma("sp", o_s[l, c["seq"]].rearrange("h n p -> n h p"), Hst[l][:], R=(Hst[l],))

    def hgrn_phase_setup(l, wi):
        a = aux[wi]
        for i in range(4):
            memset("pool", ke[i], ke[i][:], 0.0)
        if l == 1:
            tt("dve", a[:, 512:1024], a[:, 1024:1536], a[:, 512:1024], ALU.subtract, (a,), (a,))
            act(a[:, 512:1024], a[:, 512:1024], AF.Exp, (a,), (a,))
            ts("dve", a[:, 512:1024], a[:, 512:1024], 1.0, ALU.add, (a,), (a,))
            E("dve", lambda: nc.vector.reciprocal(out=a[:, 512:1024], in_=a[:, 512:1024]), (a,), (a,))

    def hgrn_chunk(c, l, wi):
        L = c["L"]; a = aux[wi]
        NB = (L + 31) // 32
        blocks = [(i * 32, min(L, (i + 1) * 32)) for i in range(NB)]
        s0, s1, s2, s3 = smsets[alt()]
        project(c, wi, 512, 512, banks[1])
        act(e2[:L, :], banks[1][:L, :], AF.Exp, (banks[1],), (e2,))
        act(e2[:L, :], e2[:L, :], AF.Ln, (e2,), (e2,), bias=1.0)
        act(kf[:L, :], e2[:L, :], AF.Exp, (e2,), (kf,), scale=-1.0)
        if l == 1:
            tt("dve", kf[:L, :], kf[:L, :], a[:L, 512:1024], ALU.mult, (kf, a), (kf,))
        act(e2[:L, :], kf[:L, :], AF.Ln, (kf,), (e2,), bias=1.0, scale=-1.0)
        hilo(e2[:L, :], L, 512, lfh, (e2,))
        kb = kb_t
        vcopy("pool", kb[:L, :], kf[:L, :], (kf,), (kb,))
        project(c, wi, 0, 512, banks[0])
        qb_ = qk_b[alt()]
        E("act", lambda b_=banks[0]: nc.scalar.activation(out=qb_[:L, :], in_=b_[:L, :], func=AF.Copy, scale=128 ** -0.5),
          (banks[0],), (qb_,))
        bb = banks[4]
        for h in range(4):
            mm(bb[:, h * 128:h * 128 + L], lfh[:L, 0, h * 128:(h + 1) * 128], tri[:L, :L], True, False, (lfh, tri), (bb,))
            mm(bb[:, h * 128:h * 128 + L], lfh[:L, 1, h * 128:(h + 1) * 128], tri[:L, :L], False, True, (lfh, tri), (bb,))
        E("act", lambda: nc.scalar.copy(out=bT[:, :, 0:L], in_=bb[:, :].rearrange("p (h t) -> p h t", t=128)[:, :, 0:L]), (bb,), (bT,))
        br = banks[5]
        mm(br[:L, :], ust[:L, :L], lfh[:L, 0, :], True, False, (ust, lfh), (br,))
        mm(br[:L, :], ust[:L, :L], lfh[:L, 1, :], False, True, (ust, lfh), (br,))
        act(e1[:L, 0:512], br[:L, :], AF.Exp, (br,), (e1,))
        kd = kd_b[alt()]
        tt("dve", kd[:L, :], e1[:L, 0:512], kf[:L, :], ALU.mult, (e1, kf), (kd,))
        bkt = banks[2]; v2 = bfv(bkt)
        for h in range(4):
            tr(v2[:, h * 128:h * 128 + L], qb_[:L, h * 128:(h + 1) * 128], ident[:L, :L], (qb_, ident), (bkt,))
            tr(v2[:, 512 + h * 128:512 + h * 128 + L], kb[:L, h * 128:(h + 1) * 128], ident[:L, :L], (kb, ident), (bkt,))
        E("act", lambda: nc.scalar.copy(out=qT_f[:, :, 0:L], in_=v2[:, 0:512].rearrange("p (h t) -> p h t", t=128)[:, :, 0:L]), (bkt,), (qT_f,))
        E("act", lambda: nc.scalar.copy(out=kT_b[:, :, 0:L], in_=v2[:, 512:1024].rearrange("p (h t) -> p h t", t=128)[:, :, 0:L]), (bkt,), (kT_b,))
        memset("pool", beta, beta[:, :, 0:1], 0.0)
        for i in range(1, NB):
            vcopy("pool", beta[:, :, i:i + 1], bT[:, :, 32 * i - 1:32 * i], (bT,), (beta,))
        for i, (t0, t1) in enumerate(blocks):
            tt("dve", dq[:, :, t0:t1], bT[:, :, t0:t1], beta[:, :, i:i + 1].to_broadcast([128, 4, t1 - t0]), ALU.subtract,
               (bT, beta), (dq,))
        act(dq[:, :, 0:L], dq[:, :, 0:L], AF.Exp, (dq,), (dq,))
        qe_ = qe[alt()]; qb2 = qb[alt()]
        tt("dve", qe_[:, :, 0:L], dq[:, :, 0:L], qT_f[:, :, 0:L], ALU.mult, (dq, qT_f), (qe_,))
        act(dq[:, :, 0:L], bT[:, :, 0:L], AF.Exp, (bT, dq), (dq,))
        tt("dve", qb2[:, :, 0:L], dq[:, :, 0:L], qT_f[:, :, 0:L], ALU.mult, (dq, qT_f), (qb2,))
        for i, (t0, t1) in enumerate(blocks):
            tt("pool", dk[:, :, 0:t1], beta[:, :, i:i + 1].to_broadcast([128, 4, t1]), bT[:, :, 0:t1], ALU.subtract, (beta, bT), (dk,))
            act(dk[:, :, 0:t1], dk[:, :, 0:t1], AF.Exp, (dk,), (dk,))
            tt("dve", ke[i][:, :, 0:t1], dk[:, :, 0:t1], kT_b[:, :, 0:t1], ALU.mult, (dk, kT_b), (ke[i],))
        bs = banks[3]
        for h in range(4):
            for i, (t0, t1) in enumerate(blocks):
                mm(bs[:L, h * 128 + t0:h * 128 + t1], ke[i][:, h, 0:L], qe_[:, h, t0:t1], True, True, (ke[i], qe_), (bs,))
        pt = PT[alt()]
        tt("dve", pt[:L, :, 0:L], bs[:L, :].rearrange("p (h t) -> p h t", t=128)[:, :, 0:L],
           tri_f[:L, 0:L].unsqueeze(1).to_broadcast([L, 4, L]), ALU.mult, (bs, tri_f), (pt,))
        project(c, wi, 1024, 512, banks[0])
        vb = v_b[alt()]
        E("act", lambda b_=banks[0]: nc.scalar.copy(out=vb[:L, :, 0:128], in_=b_[:L, :].rearrange("p (h v) -> p h v", v=128)),
          (banks[0],), (vb,))
        bo = banks[6]
        for h in range(4):
            mm(bo[:L, h * 128:(h + 1) * 128], pt[:L, h, 0:L], vb[:L, h, 0:128], True, False, (pt, vb), (bo,))
            mm(bo[:L, h * 128:(h + 1) * 128], qb2[:, h, 0:L], Gbf[:, h, :], False, True, (qb2, Gbf), (bo,))
        bu = banks[7]
        for h in range(4):
            mm(bu[:, h * 128:(h + 1) * 128], kd[:L, h * 128:(h + 1) * 128], vb[:L, h, 0:128], True, True, (kd, vb), (bu,))
        Gs = Gst[l]
        act(beta[:, :, 0:1], bT[:, :, L - 1:L], AF.Exp, (bT, beta), (beta,))
        tt("dve", Gs[:], Gs[:], beta[:, :, 0:1].to_broadcast([128, 4, 128]), ALU.mult, (Gs, beta), (Gs,))
        tt("dve", Gs[:], Gs[:], bu[:, :].rearrange("p (h v) -> p h v", v=128), ALU.add, (Gs, bu), (Gs,))
        E("act", lambda: nc.scalar.copy(out=Gbf[:], in_=Gs[:]), (Gs,), (Gbf,))
        for h in range(4):
            act(junk[:L, 0:128], bo[:L, h * 128:(h + 1) * 128], AF.Square, (bo,), (junk, s3), accum=s3[:L, 24 + h:25 + h])
        act(s2[:L, 0:4], s3[:L, 24:28], AF.Ln, (s3,), (s2,), bias=EPS, scale=1.0 / 128)
        act(s2[:L, 4:8], s2[:L, 0:4], AF.Exp, (s2,), (s2,), scale=-0.5)
        gg = g1[alt()]
        for h in range(4):
            stt("dve", gg[:L, h * 128:(h + 1) * 128], bo[:L, h * 128:(h + 1) * 128], s2[:L, 4 + h:5 + h],
                a[:L, h * 128:(h + 1) * 128], ALU.mult, ALU.mult, (bo, s2, a), (gg,))
        project(c, wi, 1536, 512, banks[5])
        act(e1[:L, 0:512], banks[5][:L, :], AF.Exp, (banks[5],), (e1,), scale=-1.0)
        act(e1[:L, 0:512], e1[:L, 0:512], AF.Ln, (e1,), (e1,), bias=1.0)
        act(e1[:L, 512:1024], e1[:L, 0:512], AF.Exp, (e1,), (e1,), scale=-1.0)
        tt("dve", e1[:L, 512:1024], e1[:L, 512:1024], banks[5][:L, :], ALU.mult, (e1, banks[5]), (e1,))
        omt = om[alt()]
        tt("dve", omt[:L, :], gg[:L, :], e1[:L, 512:1024], ALU.mult, (gg, e1), (omt,))
        out_proj(c, wi, omt, (3, 7))

    def hgrn_init(c, l):
        Gs = Gst[l]
        if c["seq"] < NSEQ:
            memset("pool", Gs, Gs[:], 0.0)
            memset("pool", Gbf, Gbf[:], 0.0)
        else:
            dma("sp", Gs[:], st_h[l].rearrange("h k v -> k h v"), W=(Gs,))
            E("act", lambda: nc.scalar.copy(out=Gbf[:], in_=Gs[:]), (Gs,), (Gbf,))

    def hgrn_final(c, l):
        dma("sp", o_h[l, c["seq"]].rearrange("h k v -> k h v"), Gst[l][:], R=(Gst[l],))

    CHUNK = {"M": mlstm_chunk, "R": ret_chunk, "S": ssd_chunk, "H": hgrn_chunk}
    INIT = {"M": mlstm_init, "R": ret_init, "S": ssd_init, "H": hgrn_init}
    SHADOW = {"M": (Cst, Cbf), "R": (Rst, Rbf), "S": (Hst, Hbf), "H": (Gst, Gbf)}

    phases = []
    for ui, unit in enumerate(units):
        for l in range(2):
            for m in MIX:
                if m in enabled:
                    phases.append((ui, l, m))
    wi_of = {}
    wi_of[0] = load_weights(phases[0][1], phases[0][2])
    last_shadow = {m: None for m in MIX}
    seq_chunks_done = {}
    for pi, (ui, l, m) in enumerate(phases):
        unit = units[ui]
        first_phase_of_layer = (m == [mm_ for mm_ in MIX if mm_ in enabled][0])
        if first_phase_of_layer:
            P.barrier()
            if l == 0:
                for c in unit:
                    src = xs if c["seq"] == NSEQ else xp[c["seq"], c["ci"] * 128:(c["ci"] + 1) * 128, :]
                    dma("sp", x_t[c["slot"]][:c["L"], :], src, W=(x_t[c["slot"]],))
            dma("sp", normw[:], w_norm[l:l + 1, :].partition_broadcast(128), W=(normw,))
            for c in unit:
                layer_norm_transpose(c)
        if pi + 1 < len(phases):
            wi_of[pi + 1] = load_weights(phases[pi + 1][1], phases[pi + 1][2])
        wi = wi_of[pi]
        P.barrier()
        if m == "S":
            ssd_phase_setup(l, wi)
        if m == "H":
            hgrn_phase_setup(l, wi)
        for c in unit:
            key = (c["seq"], l, m)
            if c["first"]:
                P.window()
                INIT[m](c, l)
            elif last_shadow[m] != (c["seq"], l):
                stt_, bf_ = SHADOW[m]
                E("act", lambda s_=stt_[l], b_=bf_: nc.scalar.copy(out=b_[:], in_=s_[:]), (stt_[l],), (bf_,))
            last_shadow[m] = (c["seq"], l)
            next_chunk()
            if m == "M":
                P.window(sched=(M_PIN is not None), pin=(M_PIN or ()))
            else:
                P.window()
            CHUNK[m](c, l, wi)
            if c["last"]:
                P.window()
                if m == "M":
                    mlstm_final(c, l, 1 if c["seq"] == NSEQ else NCT, NCT if c["seq"] == NSEQ else 0)
                elif m == "R":
                    ret_final(c, l)
                elif m == "S":
                    ssd_final(c, l)
                else:
                    hgrn_final(c, l)
        last_mixer = [mm_ for mm_ in MIX if mm_ in enabled][-1]
        if l == 1 and m == last_mixer:
            P.barrier()
            dma("sp", normw[:], w_fn[0:1, :].partition_broadcast(128), W=(normw,))
            for c in unit:
                L = c["L"]; xt = x_t[c["slot"]]
                s = sm[0]
                act(junk[:L, :], xt[:L, :], AF.Square, (xt,), (junk, s), accum=s[:L, 0:1])
                act(s[:L, 1:2], s[:L, 0:1], AF.Ln, (s,), (s,), bias=EPS, scale=1.0 / D)
                act(s[:L, 2:3], s[:L, 1:2], AF.Exp, (s,), (s,), scale=-0.5)
                stt("dve", xt[:L, :], xt[:L, :], s[:L, 2:3], normw[:L, :], ALU.mult, ALU.mult, (xt, s, normw), (xt,))
                dst = y_s if c["seq"] == NSEQ else y_p[c["seq"], c["ci"] * 128:(c["ci"] + 1) * 128, :]
                dma("sp", dst, xt[:L, :], R=(xt,))

    P.barrier()
    fin = P.emit("sp", lambda: None)

    def fin_cb():
        for e in ("sp", "pool", "act"):
            for op in P.ops[e]:
                if op.dma:
                    fin.deps.append(op)

    P.finalize(st, fin_cb)
    st.close()
    return nc


def host_consts(NCT, TS=16, PAST=1024):
    ident = np.eye(128, dtype=np.float32)
    r = np.arange(128)
    tri = (r[:, None] <= r[None, :]).astype(np.float32)
    ust = (r[:, None] > r[None, :]).astype(np.float32)
    half = 32
    freqs = (10000.0 ** (-np.arange(half, dtype=np.float32) / half)).astype(np.float32)
    cos = np.zeros((128, NCT + 1, 32), np.float32); sin = np.zeros((128, NCT + 1, 32), np.float32)
    for ci in range(NCT):
        pos = (ci * 128 + r).astype(np.float32)
        ang = pos[:, None] * freqs[None, :]
        cos[:, ci] = np.cos(ang); sin[:, ci] = np.sin(ang)
    pos = (PAST + np.arange(TS)).astype(np.float32)
    ang = pos[:, None] * freqs[None, :]
    cos[:TS, NCT] = np.cos(ang); sin[:TS, NCT] = np.sin(ang)
    lg = np.log1p(-(2.0 ** (-5.0 - np.arange(4, dtype=np.float64))))
    gqk = np.zeros((128, 8), np.float32)
    gqk[:, 0:4] = np.exp((r[:, None] + 1.0) * lg[None, :]) * (64 ** -0.5)
    gqk[:, 4:8] = np.exp(-(r[:, None] + 1.0) * lg[None, :])
    gl = np.zeros((64, 8), np.float32)
    gl[:, 0:4] = np.exp(128.0 * lg)[None, :]
    gl[:, 4:8] = np.exp(float(TS) * lg)[None, :]
    return dict(c_id=ident, c_tri=tri, c_ust=ust, c_cos=cos, c_sin=sin, c_gqk=gqk, c_gl=gl)


_CACHE = {}


def run_cores(per_core_inputs, NSEQ, TP, NCH, enabled=MIX, core_ids=None):
    key = (NSEQ, TP, NCH, tuple(enabled))
    if key not in _CACHE:
        _CACHE[key] = build_program(NSEQ, TP, NCH, enabled=enabled)
    nc = _CACHE[key]
    if core_ids is None:
        core_ids = list(range(len(per_core_inputs)))
    res = run_bass_kernel_spmd(nc, per_core_inputs, core_ids=core_ids)
    return res.results


def make_core_inputs(xp, xs, states, weights, NCT):
    f = lambda a: np.ascontiguousarray(a, dtype=np.float32)
    m = dict(xp=f(xp), xs=f(xs))
    m.update({k: f(v) for k, v in states.items()})
    m.update({k: f(v) for k, v in weights.items()})
    m.update(host_consts(NCT))
    return m


def kernel(x_prompt, x_sample, state_mlstm_c, state_mlstm_n, state_mlstm_m, state_ret, state_ssd,
           cache_ssd_conv, state_hgrn, norm_w, w_in, mlstm_gate_b, mlstm_norm_w, ssd_conv_w, ssd_conv_b,
           ssd_dt_bias, ssd_a_log, ssd_d, ssd_norm_w, hgrn_lower_bounds, hgrn_norm_w, w_out, final_norm_w):
    x_prompt = np.asarray(x_prompt); x_sample = np.asarray(x_sample)
    B, TP, _ = x_prompt.shape
    NCORES = 8
    NSEQ = B // NCORES
    NCT = TP // 128
    weights = dict(w_norm=norm_w, w_in=w_in, w_gb=mlstm_gate_b, w_mnw=mlstm_norm_w, w_cw=ssd_conv_w, w_cb=ssd_conv_b,
                   w_dtb=ssd_dt_bias, w_alog=ssd_a_log, w_sd=ssd_d, w_snw=ssd_norm_w, w_lb=hgrn_lower_bounds,
                   w_hnw=hgrn_norm_w, w_out=w_out, w_fn=np.asarray(final_norm_w).reshape(1, -1))
    weights = {k: np.asarray(v) for k, v in weights.items()}
    ins = []
    for cidx in range(NCORES):
        states = dict(st_mc=np.asarray(state_mlstm_c)[:, cidx], st_mn=np.asarray(state_mlstm_n)[:, cidx],
                      st_mm=np.asarray(state_mlstm_m)[:, cidx], st_r=np.asarray(state_ret)[:, cidx],
                      st_s=np.asarray(state_ssd)[:, cidx], st_cv=np.asarray(cache_ssd_conv)[:, cidx],
                      st_h=np.asarray(state_hgrn)[:, cidx])
        ins.append(make_core_inputs(x_prompt[cidx * NSEQ:(cidx + 1) * NSEQ], x_sample[cidx], states, weights, NCT))
    res = run_cores(ins, NSEQ, TP, NCH=4)
    cat = lambda k, sl: np.concatenate([r[k][sl] for r in res], axis=0)
    y_prompt = np.concatenate([r["y_p"] for r in res], axis=0)
    y_sample = np.stack([r["y_s"] for r in res], axis=0)

    def pst(k):
        return np.concatenate([r[k][:, 0:NSEQ] for r in res], axis=1)

    def sst(k):
        return np.concatenate([r[k][:, NSEQ:NSEQ + 1] for r in res], axis=1)

    names = ["o_mc", "o_mn", "o_mm", "o_r", "o_s", "o_cv", "o_h"]
    outs = [y_prompt, y_sample] + [pst(k) for k in names] + [sst(k) for k in names]
    return tuple(np.ascontiguousarray(o, dtype=np.float32) for o in outs)
```
